# Optimizing a Trainium2 kernel written in Bass

```python
import math
import jax, jax.numpy as jnp
from jax import lax
import numpy as np

D_MODEL = 1024
BATCH = 2
SEQ = 8192
DEPTH = 2
DEC_BATCH = 128
DEC_SEQ = 8
PAST_LEN = 2048
PAGE_SIZE = 128

CONV_CH = D_MODEL // 2
N_CONV_GROUPS = 8
CONV_K = 3
N_HEADS = 4
QK_DIM = 64
V_DIM = 2 * QK_DIM
ATTN_W = N_HEADS * V_DIM
QK_W = N_HEADS * 2 * QK_DIM
MIX_W = CONV_CH + ATTN_W
D_IN = 3 * CONV_CH + 2 * QK_W + ATTN_W
SPLITS = (CONV_CH, 2 * CONV_CH, 3 * CONV_CH, 3 * CONV_CH + QK_W, 3 * CONV_CH + 2 * QK_W)
D_FF = 2816
NUM_BUCKETS = 32
MAX_DISTANCE = 128
Q_BLOCK = 128
LN_EPS = 1e-5
ATTN_SCALE = QK_DIM ** -0.5
ALPHA = (2 * DEPTH) ** 0.25
BETA = (8 * DEPTH) ** -0.25

kernel_name = "hymba_conv_diffattn_macaron_deepnorm_step"


def layer_norm(x, g, b):
    xf = x.astype(jnp.float32)
    mu = jnp.mean(xf, axis=-1, keepdims=True)
    var = jnp.mean(jnp.square(xf - mu), axis=-1, keepdims=True)
    return ((xf - mu) * lax.rsqrt(var + LN_EPS) * g + b).astype(x.dtype)


def swiglu(x, wg, wu, wd):
    return (jax.nn.silu(x @ wg) * (x @ wu)) @ wd


def rel_bucket(dist):
    n = jnp.maximum(dist, 0)
    max_exact = NUM_BUCKETS // 2
    nf = jnp.maximum(n, 1).astype(jnp.float32)
    large = max_exact + (jnp.log(nf / max_exact) / math.log(MAX_DISTANCE / max_exact)
                         * (NUM_BUCKETS - max_exact)).astype(jnp.int32)
    large = jnp.minimum(large, NUM_BUCKETS - 1)
    return jnp.where(n < max_exact, n, large)


def rel_bias(table, q_pos, k_pos):
    buckets = rel_bucket(q_pos[:, None] - k_pos[None, :])
    return jnp.moveaxis(table[buckets], -1, 0).astype(jnp.float32)


def lambda_full(lq1, lk1, lq2, lk2, lam_init):
    f32 = jnp.float32
    return (jnp.exp(jnp.sum(lq1.astype(f32) * lk1.astype(f32)))
            - jnp.exp(jnp.sum(lq2.astype(f32) * lk2.astype(f32))) + lam_init)


def diff_logits(q, k):
    return jnp.einsum('bqhcd,bkhcd->bhcqk', q, k).astype(jnp.float32) * ATTN_SCALE


def short_conv(u, prev, w):
    full = jnp.concatenate([prev, u], axis=1)
    t = u.shape[1]
    y = w[0] * full[:, 0:t]
    for j in range(1, CONV_K):
        y = y + w[j] * full[:, j:j + t]
    return y, full[:, -(CONV_K - 1):]


def prompt_attention(q, k, v, lam, table):
    b, s = q.shape[0], q.shape[1]
    nblk = s // Q_BLOCK
    qb = q.reshape(b, nblk, Q_BLOCK, N_HEADS, 2, QK_DIM).swapaxes(0, 1)
    k_pos = jnp.arange(s)

    def block(args):
        qi, bi = args
        q_pos = bi * Q_BLOCK + jnp.arange(Q_BLOCK)
        logits = diff_logits(qi, k) + rel_bias(table, q_pos, k_pos)[None, :, None]
        logits = jnp.where(k_pos[None, :] <= q_pos[:, None], logits, -jnp.inf)
        p = jax.nn.softmax(logits, axis=-1)
        a = (p[:, :, 0] - lam * p[:, :, 1]).astype(v.dtype)
        return jnp.einsum('bhqk,bkhd->bqhd', a, v)

    o = lax.map(block, (qb, jnp.arange(nblk)))
    return o.swapaxes(0, 1).reshape(b, s, N_HEADS, V_DIM)


def sample_attention(q, k_new, v_new, k_past, v_past, lam, table):
    t = q.shape[1]
    past = k_past.shape[1]
    pos_new = past + jnp.arange(t)
    lp = diff_logits(q, k_past) + rel_bias(table, pos_new, jnp.arange(past))[None, :, None]
    ln = diff_logits(q, k_new) + rel_bias(table, pos_new, pos_new)[None, :, None]
    ln = jnp.where(pos_new[None, :] <= pos_new[:, None], ln, -jnp.inf)
    p = jax.nn.softmax(jnp.concatenate([lp, ln], axis=-1), axis=-1)
    a = (p[:, :, 0] - lam * p[:, :, 1]).astype(v_new.dtype)
    return (jnp.einsum('bhqk,bkhd->bqhd', a[..., :past], v_past)
            + jnp.einsum('bhqk,bkhd->bqhd', a[..., past:], v_new))


def head_out(o, subln_w, lam_init, dtype):
    of = o.astype(jnp.float32)
    of = of * lax.rsqrt(jnp.mean(jnp.square(of), axis=-1, keepdims=True) + LN_EPS)
    of = of * subln_w * (1.0 - lam_init)
    return of.reshape(o.shape[0], o.shape[1], ATTN_W).astype(dtype)


def layer_forward(x, conv_prev, attn_core, lam_init, w_in, w_out, conv_w, subln_w,
                  ln_g, ln_b, f1g, f1u, f1d, f2g, f2u, f2d):
    x = layer_norm(ALPHA * x + 0.5 * swiglu(x, f1g, f1u, f1d), ln_g[0], ln_b[0])
    b, t, _ = x.shape
    h = x @ w_in
    bg, cg, xin, q, k, v = jnp.split(h, SPLITS, axis=-1)
    y_conv, conv_state = short_conv(cg * xin, conv_prev, conv_w)
    z_conv = bg * y_conv
    q = q.reshape(b, t, N_HEADS, 2, QK_DIM)
    k = k.reshape(b, t, N_HEADS, 2, QK_DIM)
    v = v.reshape(b, t, N_HEADS, V_DIM)
    z_attn = head_out(attn_core(q, k, v), subln_w, lam_init, x.dtype)
    mix = jnp.concatenate([z_conv, z_attn], axis=-1) @ w_out
    x = layer_norm(ALPHA * x + mix, ln_g[1], ln_b[1])
    x = layer_norm(ALPHA * x + 0.5 * swiglu(x, f2g, f2u, f2d), ln_g[2], ln_b[2])
    return x, k.reshape(b, t, N_HEADS, 2 * QK_DIM), v, conv_state


def setup_inputs(seed: int = 0) -> dict:
    key = jax.random.key(seed)
    ks = jax.random.split(key, 24)
    f32 = jnp.float32
    n_pages = PAST_LEN // PAGE_SIZE
    n_phys = (DEC_BATCH * n_pages * 5) // 4
    nrm = lambda k, shape, s: jax.random.normal(k, shape, f32) * s
    page_table = jax.random.permutation(ks[5], n_phys)[:DEC_BATCH * n_pages]
    page_table = page_table.reshape(DEC_BATCH, n_pages).astype(jnp.int32)
    return {
        "x_prompt": nrm(ks[0], (BATCH, SEQ, D_MODEL), 1.0),
        "x_sample": nrm(ks[1], (DEC_BATCH, DEC_SEQ, D_MODEL), 1.0),
        "cache_k": nrm(ks[2], (DEPTH, n_phys, PAGE_SIZE, N_HEADS, 2 * QK_DIM), 1.0),
        "cache_v": nrm(ks[3], (DEPTH, n_phys, PAGE_SIZE, N_HEADS, V_DIM), 1.0),
        "state_conv": nrm(ks[4], (DEPTH, DEC_BATCH, CONV_K - 1, CONV_CH), 1.0),
        "page_table": page_table,
        "rel_bias_table": nrm(ks[6], (NUM_BUCKETS, N_HEADS), 0.5),
        "w_in": nrm(ks[7], (DEPTH, D_MODEL, D_IN), D_MODEL ** -0.5),
        "w_out": nrm(ks[8], (DEPTH, MIX_W, D_MODEL), BETA * MIX_W ** -0.5),
        "conv_w": nrm(ks[9], (DEPTH, CONV_K, CONV_CH), CONV_K ** -0.5),
        "lambda_q1": nrm(ks[10], (DEPTH, QK_DIM), 0.1),
        "lambda_k1": nrm(ks[11], (DEPTH, QK_DIM), 0.1),
        "lambda_q2": nrm(ks[12], (DEPTH, QK_DIM), 0.1),
        "lambda_k2": nrm(ks[13], (DEPTH, QK_DIM), 0.1),
        "subln_w": 1.0 + nrm(ks[14], (DEPTH, V_DIM), 0.01),
        "ln_g": 1.0 + nrm(ks[15], (DEPTH, 3, D_MODEL), 0.01),
        "ln_b": nrm(ks[16], (DEPTH, 3, D_MODEL), 0.01),
        "ffn1_w_gate": nrm(ks[17], (DEPTH, D_MODEL, D_FF), D_MODEL ** -0.5),
        "ffn1_w_up": nrm(ks[18], (DEPTH, D_MODEL, D_FF), D_MODEL ** -0.5),
        "ffn1_w_down": nrm(ks[19], (DEPTH, D_FF, D_MODEL), BETA * D_FF ** -0.5),
        "ffn2_w_gate": nrm(ks[20], (DEPTH, D_MODEL, D_FF), D_MODEL ** -0.5),
        "ffn2_w_up": nrm(ks[21], (DEPTH, D_MODEL, D_FF), D_MODEL ** -0.5),
        "ffn2_w_down": nrm(ks[22], (DEPTH, D_FF, D_MODEL), BETA * D_FF ** -0.5),
    }


def reference(x_prompt, x_sample, cache_k, cache_v, state_conv, page_table, rel_bias_table,
              w_in, w_out, conv_w, lambda_q1, lambda_k1, lambda_q2, lambda_k2, subln_w,
              ln_g, ln_b, ffn1_w_gate, ffn1_w_up, ffn1_w_down,
              ffn2_w_gate, ffn2_w_up, ffn2_w_down):
    xp, xs = x_prompt, x_sample
    n_dec = x_sample.shape[0]
    kp_rows, vp_rows, cp_states = [], [], []
    ks_rows, vs_rows, cs_states = [], [], []
    for l in range(DEPTH):
        lam_init = 0.8 - 0.6 * math.exp(-0.3 * l)
        lam = lambda_full(lambda_q1[l], lambda_k1[l], lambda_q2[l], lambda_k2[l], lam_init)
        wts = (w_in[l], w_out[l], conv_w[l], subln_w[l], ln_g[l], ln_b[l],
               ffn1_w_gate[l], ffn1_w_up[l], ffn1_w_down[l],
               ffn2_w_gate[l], ffn2_w_up[l], ffn2_w_down[l])

        def prompt_core(q, k, v, lam=lam):
            return prompt_attention(q, k, v, lam, rel_bias_table)
        conv0 = jnp.zeros((xp.shape[0], CONV_K - 1, CONV_CH), xp.dtype)
        xp, kp, vp, cp = layer_forward(xp, conv0, prompt_core, lam_init, *wts)
        kp_rows.append(kp); vp_rows.append(vp); cp_states.append(cp)

        k_past = cache_k[l][page_table].reshape(n_dec, -1, N_HEADS, 2, QK_DIM)
        v_past = cache_v[l][page_table].reshape(n_dec, -1, N_HEADS, V_DIM)

        def sample_core(q, k, v, lam=lam, k_past=k_past, v_past=v_past):
            return sample_attention(q, k, v, k_past, v_past, lam, rel_bias_table)
        xs, ksn, vsn, csn = layer_forward(xs, state_conv[l], sample_core, lam_init, *wts)
        ks_rows.append(ksn); vs_rows.append(vsn); cs_states.append(csn)

    k_prompt = jnp.stack(kp_rows)
    v_prompt = jnp.stack(vp_rows)
    conv_prompt = jnp.stack(cp_states)
    k_sample = jnp.stack(ks_rows)
    v_sample = jnp.stack(vs_rows)
    conv_sample = jnp.stack(cs_states)
    return (xp, xs, k_prompt, v_prompt, conv_prompt, k_sample, v_sample, conv_sample)
```

```python
import contextlib
import math
import numpy as np
import concourse.bass as bass
import concourse.mybir as mybir
from concourse.bass_utils import run_bass_kernel_spmd

F32 = mybir.dt.float32
BF16 = mybir.dt.bfloat16
I32 = mybir.dt.int32
AF = mybir.ActivationFunctionType
ALU = mybir.AluOpType

PE, ACT, DVE, POOL, SP = "pe", "act", "dve", "pool", "sp"
COMPUTE = (PE, ACT, DVE, POOL)
TICKS_PER_SEM = 1000000
DMA_USES_PER_SEM = 100000
NSEM_DMA = 8

D = 1024
KC = 8
DEPTH = 2
LN_EPS = 1e-5
ALPHA = (2 * DEPTH) ** 0.25
NEG = -30000.0
FULL = dict(C=512, NS=4, R=4, NG=2, BPC=16, NPG=16, DFF=2816, NPHYS=2560)


class StopBuild(Exception):
    pass


class Rec:
    def __init__(self, nc):
        self.nc = nc
        self.ops = []
        self.last_w = {}
        self.readers = {}
        self.bar = None
        self.bar_done = set()
        self.since = {}
        self.dmas_since = set()

    def barrier(self):
        self.bar = set(self.since.values()) | set(self.dmas_since)
        self.bar_done = set()
        self.dmas_since = set()

    def add(self, eng, fn, reads=(), writes=(), dma=False):
        i = len(self.ops)
        deps = set()
        psr = [t for t in reads if t.startswith("ps") and t[2:].isdigit()]
        if psr:
            writes = list(writes) + [t for t in psr if t not in writes]
        if self.bar is not None and eng not in self.bar_done:
            deps.update(self.bar)
            self.bar_done.add(eng)
        if dma:
            self.dmas_since.add(i)
        else:
            self.since[eng] = i
        for t in list(reads) + list(writes):
            w = self.last_w.get(t)
            if w is not None:
                deps.add(w)
        for t in writes:
            rd = self.readers.get(t)
            if rd:
                deps.update(rd[0].values())
                deps.update(rd[1])
        for t in writes:
            self.last_w[t] = i
            self.readers[t] = ({}, set())
        for t in reads:
            rd = self.readers.setdefault(t, ({}, set()))
            if dma:
                rd[1].add(i)
            else:
                rd[0][eng] = i
        self.ops.append(dict(eng=eng, fn=fn, deps=deps, dma=dma))
        return i

    def emit(self):
        nc = self.nc
        ops = self.ops
        engs = {PE: nc.tensor, ACT: nc.scalar, DVE: nc.vector, POOL: nc.gpsimd, SP: nc.sync}
        signal = [False] * len(ops)
        for i, op in enumerate(ops):
            for j in op["deps"]:
                d = ops[j]
                if d["dma"]:
                    continue
                if d["eng"] == op["eng"] and not op["dma"] and op["eng"] == PE:
                    continue
                signal[j] = True
        tick = {}
        cnt = {e: 0 for e in COMPUTE}
        dman = {}
        dcnt = {}
        for i, op in enumerate(ops):
            if op["dma"] == "cc":
                pass
            elif op["dma"]:
                q = op["eng"]
                dman[i] = dcnt.get(q, 0)
                dcnt[q] = dcnt.get(q, 0) + 1
            elif signal[i]:
                e = op["eng"]
                tick[i] = cnt[e]
                cnt[e] += 1
        sems = {}
        stack = contextlib.ExitStack()

        def getsem(key):
            s = sems.get(key)
            if s is None:
                s = stack.enter_context(nc.semaphore("s_" + "_".join(str(k) for k in key)))
                sems[key] = s
            return s

        def csem(e, t):
            return getsem(("c", e, t // TICKS_PER_SEM)), (t % TICKS_PER_SEM) + 1

        def dsem(q, n):
            gen = n // (NSEM_DMA * DMA_USES_PER_SEM)
            slot = n % NSEM_DMA
            use = (n % (NSEM_DMA * DMA_USES_PER_SEM)) // NSEM_DMA
            return getsem(("d", q, gen, slot)), 16 * (use + 1), use

        ccl = [j for j, o in enumerate(ops) if o["dma"] == "cc"]
        CC_SEMS = max(1, (len(ccl) + 1) // 2)
        ccidx = {j: n for n, j in enumerate(ccl)}

        def asem(j):
            if ops[j]["dma"] == "cc":
                k = ccidx[j]
                return getsem(("cc", k % CC_SEMS)), k // CC_SEMS + 1, 0
            return dsem(ops[j]["eng"], dman[j])

        for i, op in enumerate(ops):
            if op["dma"]:
                asem(i)
            elif signal[i]:
                csem(op["eng"], tick[i])

        def emit_engine(e):
            eng = engs[e]
            waited_c = {}
            waited_d = {}
            for i, op in enumerate(ops):
                if op["eng"] != e:
                    continue
                need_c = {}
                need_d = {}
                for j in op["deps"]:
                    d = ops[j]
                    if d["dma"]:
                        s, v, _ = asem(j)
                        if waited_d.get(id(s), 0) >= v:
                            continue
                        if need_d.get(id(s), (None, 0))[1] < v:
                            need_d[id(s)] = (s, v)
                    else:
                        if d["eng"] == e and e == PE and not op["dma"]:
                            continue
                        t = tick[j]
                        if waited_c.get(d["eng"], -1) >= t:
                            continue
                        if need_c.get(d["eng"], -1) < t:
                            need_c[d["eng"]] = t
                if op["dma"]:
                    s, v, use = asem(i)
                    if use > 0 and waited_d.get(id(s), 0) < v - 16:
                        eng.wait_ge(s, v - 16)
                        waited_d[id(s)] = v - 16
                for de, t in need_c.items():
                    s, v = csem(de, t)
                    eng.wait_ge(s, v)
                    waited_c[de] = t
                for k, (s, v) in need_d.items():
                    eng.wait_ge(s, v)
                    waited_d[k] = v
                ins = op["fn"](eng)
                if op["dma"] == "cc":
                    s, v, _ = asem(i)
                    ins.then_inc(s, 1)
                elif op["dma"]:
                    s, v, _ = asem(i)
                    ins.then_inc(s, 16)
                elif signal[i]:
                    s, v = csem(e, tick[i])
                    ins.then_inc(s, 1)
            last = {}
            for i, op in enumerate(ops):
                if op["dma"] and op["eng"] == e:
                    s, v, _ = asem(i)
                    last[id(s)] = (s, v)
            for s, v in last.values():
                eng.wait_ge(s, v)

        with nc.Block() as block:
            @block.sync
            def _(x):
                emit_engine(SP)

            @block.tensor
            def _(x):
                emit_engine(PE)

            @block.scalar
            def _(x):
                emit_engine(ACT)

            @block.vector
            def _(x):
                emit_engine(DVE)

            @block.gpsimd
            def _(x):
                emit_engine(POOL)
        stack.close()


def build(cfg):
    C, NS, R, BPC, NPG, DFF, NPHYS = (cfg[k] for k in ("C", "NS", "R", "BPC", "NPG", "DFF", "NPHYS"))
    T = 8
    NSMP = BPC * T
    NTP = NS * C
    NTOK = NTP + NSMP
    TPC = C // 128
    NT = R * TPC
    WW = C + 128 * NT
    WLEN = ((WW + 127 + 127) // 128) * 128
    FCH = DFF // 128
    FB = 256
    NFB = DFF // FB
    NB = R * BPC
    XKW = NTP + 2 * NSMP
    groups = [(s * C, C) for s in range(NS)] + [(NTP, NSMP)]
    WMAX = max(C, NSMP)

    nc = bass.Bass("TRN2", target_bir_lowering=False)
    st = contextlib.ExitStack()

    def din(name, shape, dt=F32):
        return nc.dram_tensor(name, list(shape), dt, kind="ExternalInput").ap()

    def dout(name, shape, dt=F32):
        return nc.dram_tensor(name, list(shape), dt, kind="ExternalOutput").ap()

    xT_in = din("xT_in", [D, NTOK])
    wg = [din("wg1", [DEPTH, D, DFF]), din("wg2", [DEPTH, D, DFF])]
    wu = [din("wu1", [DEPTH, D, DFF]), din("wu2", [DEPTH, D, DFF])]
    wd = [din("wd1", [DEPTH, DFF, D]), din("wd2", [DEPTH, DFF, D])]
    w_in = din("w_in", [DEPTH, D, 3072])
    w_out = din("w_out", [DEPTH, D, D])
    lng = din("lng", [DEPTH, 3, 128, KC])
    lnb = din("lnb", [DEPTH, 3, 128, KC])
    convw = din("convw", [DEPTH, 128, 4, 3])
    sconv = din("sconv", [DEPTH, 128, 4, BPC, 2])
    lamp = din("lamp", [DEPTH, 4 * 64])
    subw = din("subw", [DEPTH, 128])
    tab = din("tab", [32, 4])
    tabown = din("tabown", [32, 1])
    sel = din("sel", [33, WLEN])
    sels = din("sels", [33, 512])
    antiI = din("antiI", [128, 128])
    ident = din("ident", [128, 128])
    ptab = din("ptab", [1, NB * NPG], I32)
    rsel = din("rsel", [128, 4])
    hsel = din("hsel", [128, NS * R * NS])
    iotap = din("iotap", [128, 1])
    ck = [din(f"ck{l}", [NPHYS * 128, 128]) for l in range(DEPTH)]
    cv = [din(f"cv{l}", [NPHYS * 128, 128]) for l in range(DEPTH)]

    yT = dout("yT", [D, NTOK])
    ko = dout("ko", [DEPTH, NTOK, 512])
    vo = dout("vo", [DEPTH, NTOK, 512])
    convp = dout("convp", [DEPTH, NS, 512, 2])
    convs = dout("convs", [DEPTH, 512, BPC, 2])

    bgs = nc.dram_tensor("bgs", [4, 128, NTOK], F32)
    us = nc.dram_tensor("us", [4, 128, NTOK], F32)
    qs = nc.dram_tensor("qs", [4, 128, NTP], BF16)
    fscr = nc.dram_tensor("fscr", [4, WLEN], F32)
    fscr_s = nc.dram_tensor("fscr_s", [1, 512], F32)
    xk = [[nc.dram_tensor(f"xk{l}_{s}", [512, C if s < NS else 2 * NSMP], BF16) for s in range(NS + 1)]
          for l in range(DEPTH)]
    xk_all = [[nc.dram_tensor(f"xk_all{l}_{s}", [R * 512, C if s < NS else 2 * NSMP], BF16) for s in range(NS + 1)]
              for l in range(DEPTH)]
    xv = [[nc.dram_tensor(f"xv{l}_{s}", [C if s < NS else NSMP, 512], BF16) for s in range(NS + 1)]
          for l in range(DEPTH)]
    xv_all = [[nc.dram_tensor(f"xv_all{l}_{s}", [R * (C if s < NS else NSMP), 512], BF16) for s in range(NS + 1)]
              for l in range(DEPTH)]
    xt = [nc.dram_tensor(f"xt{l}", [NS, 1024], F32) for l in range(DEPTH)]
    xt_all = [nc.dram_tensor(f"xt_all{l}", [R * NS, 1024], F32) for l in range(DEPTH)]
    xz = [nc.dram_tensor(f"xz{l}", [128, NB * T], BF16) for l in range(DEPTH)]
    xz_all = [nc.dram_tensor(f"xz_all{l}", [R * 128, NB * T], BF16) for l in range(DEPTH)]

    def sb(name, shape, dt):
        return st.enter_context(nc.sbuf_tensor(name, list(shape), dt))

    xT = sb("xT", [128, KC, NTOK], F32)
    zTa = sb("zTa", [128, 4, NTOK], BF16)
    onesm = sb("onesm", [128, 128], BF16)
    identS = sb("identS", [128, 128], F32)
    antiS = sb("antiS", [128, 128], F32)
    lngS = sb("lngS", [128, DEPTH * 3 * KC], F32)
    lnbS = sb("lnbS", [128, DEPTH * 3 * KC], F32)
    convwS = sb("convwS", [128, DEPTH * 12], F32)
    b31 = sb("b31", [128, 4], F32)
    b31o = sb("b31o", [128, 1], F32)
    rselS = sb("rselS", [128, 4], F32)
    hselS = sb("hselS", [128, NS * R * NS], F32)
    iotaS = sb("iotaS", [128, 1], F32)
    IDX = sb("IDX", [128, NB * NPG], I32)
    xtl = sb("xtl", [128, R * NS, 8], F32)
    lamS = sb("lamS", [128, 8], F32)
    swS = sb("swS", [128, 128], F32)
    BL = sb("BL", [128, 2 * T], F32)
    Bn = sb("Bn", [128, 2 * T], F32)
    finc = sb("finc", [128, 8], F32)
    lamt = sb("lamt", [128, 256], F32)

    ARENA = cfg.get("ARENA", 107 * 1024)
    arena = sb("arena", [128, ARENA // 4], F32)

    class Carver:
        def __init__(self):
            self.off = 0

        def __call__(self, name, shape, dt):
            P = shape[0]
            n = 1
            for d_ in shape[1:]:
                n *= d_
            esz = 2 if dt == BF16 else 4
            nbytes = ((n * esz + 31) // 32) * 32
            assert self.off + nbytes <= ARENA, (name, self.off, nbytes, ARENA)
            ap = arena[0:P, self.off // 4:(self.off + nbytes) // 4]
            if dt != F32:
                ap = ap.bitcast(dt)
            ap = ap[:, 0:n]
            if len(shape) == 3:
                ap = ap.rearrange("p (a b) -> p a b", a=shape[1])
            elif len(shape) == 4:
                ap = ap.rearrange("p (a b c) -> p a b c", a=shape[1], b=shape[2])
            self.off += nbytes
            return ap

    ca = Carver()
    selS = ca("selS", [33, WLEN], F32)
    tabS = ca("tabS", [33, 4], F32)
    tabo = ca("tabo", [33, 1], F32)
    Fsb = ca("Fsb", [4, WLEN], F32)
    HkS = ca("HkS", [128, 2 * T], F32)
    idxf = ca("idxf", [128, NB * NPG], F32)
    ptB = ca("ptB", [128, NB * NPG], I32)
    ca = Carver()
    xb = ca("xb", [128, KC, WMAX], BF16)
    hT = ca("hT", [128, max(FCH, 2 * KC), WMAX], BF16)
    wbig = [ca(f"wbig{i}", [128, KC, 512], BF16) for i in range(2)]
    wdb = [ca(f"wdb{i}", [128, 2, 512], BF16) for i in range(3)]
    sgt = [ca(f"sgt{i}", [128, WMAX], F32) for i in range(2)]
    stat = [ca(f"stat{i}", [128, WMAX], F32) for i in range(3)]
    st4 = ca("st4", [128, 4, WMAX], F32)
    cgt = ca("cgt", [128, 4, WMAX], F32)
    stb = ca("stb", [128, 4, WMAX], BF16)
    kvst = [ca(f"kvst{i}", [128, 512], F32) for i in range(2)]
    kvb = [ca(f"kvb{i}", [128, 512], BF16) for i in range(2)]
    halo = ca("halo", [128, 4, 2], F32)
    ubuf = ca("ubuf", [128, 4, WMAX + 2 * max(1, BPC)], F32)
    zc = ca("zc", [128, 4, WMAX], BF16)
    dense_top = ca.off
    ca = Carver()
    KTh = ca("KTh", [128, R * NS * C], BF16)
    Vh = ca("Vh", [128, R * NS * TPC, 130], BF16)
    QTh = ca("QTh", [128, NTP], BF16)
    Wwin = ca("Wwin", [128, WW], F32)
    Hk = ca("Hk", [128, 512], F32)
    Pt = [ca(f"Pt{i}", [128, C], BF16) for i in range(4)]
    Stmp = [ca(f"Stmp{i}", [128, C], F32) for i in range(2)]
    fin = [ca(f"fin{i}", [128, 128], F32) for i in range(3)]
    Kraw = [ca(f"Kraw{i}", [128, NPG, 128], F32) for i in range(2)]
    Vraw = [ca(f"Vraw{i}", [128, NPG, 128], F32) for i in range(1)]
    KTs = ca("KTs", [128, NPG * 128], BF16)
    Vs = ca("Vs", [128, NPG + 1, 130], BF16)
    Qbd = ca("Qbd", [128, NB, 2 * T], BF16)
    KnT = ca("KnT", [128, NB * T], BF16)
    Ps = ca("Ps", [128, (NPG + 1) * 2 * T], BF16)
    zsT = ca("zsT", [128, NB * T], BF16)
    SQt = [ca(f"SQt{i}", [128, R, 2 * NSMP], BF16) for i in range(2)]
    Qsel = ca("Qsel", [128, R, 2 * NSMP], BF16)
    Vn4 = [ca(f"Vn4{i}", [T, 4, 128], BF16) for i in range(2)]
    zall = ca("zall", [128, 4, R, NSMP], BF16)
    attn_top = ca.off

    ps = [st.enter_context(nc.psum_tensor(f"ps{i}", [128, 512], F32)) for i in range(8)]

    r = Rec(nc)
    reg_holder = {}

    def spreg(e):
        if "r" not in reg_holder:
            reg_holder["r"] = e.alloc_register("dynreg")
        return reg_holder["r"]

    def dma(q, out, in_, reads, writes, nonc=False):
        if nonc:
            def fn(e):
                with nc.allow_non_contiguous_dma(reason="small strided"):
                    return e.dma_start(out=out, in_=in_)
        else:
            def fn(e):
                return e.dma_start(out=out, in_=in_)
        r.add(q, fn, reads=reads, writes=writes, dma=True)

    def dyn_dma(out, in_fn, idx_ap, reads, writes, nonc=False):
        def fn(e):
            reg_holder["n"] = reg_holder.get("n", 0) + 1
            reg = e.alloc_register(f"dyn{reg_holder['n']}")
            e.reg_load(reg, idx_ap)
            v = e.snap(reg)
            if nonc:
                with nc.allow_non_contiguous_dma(reason="small strided"):
                    ins = e.dma_start(out=out, in_=in_fn(v))
            else:
                ins = e.dma_start(out=out, in_=in_fn(v))
            e.free_register(reg)
            return ins
        r.add(SP, fn, reads=reads, writes=writes, dma=True)

    def mm(out, lhsT, rhs, start, stop, reads, writes):
        r.add(PE, lambda e: e.matmul(out, lhsT=lhsT, rhs=rhs, start=start, stop=stop), reads=reads, writes=writes)

    def act(out, in_, func, reads, writes, scale=None, bias=None, accum=None):
        kw = {}
        if scale is not None:
            kw["scale"] = scale
        if bias is not None:
            kw["bias"] = bias
        if accum is not None:
            kw["accum_out"] = accum
        r.add(ACT, lambda e: e.activation(out=out, in_=in_, func=func, **kw), reads=reads, writes=writes)

    def tt(eng, out, in0, in1, op, reads, writes):
        r.add(eng, lambda e: e.tensor_tensor(out=out, in0=in0, in1=in1, op=op), reads=reads, writes=writes)

    def ts(eng, out, in0, s1, op0, reads, writes, s2=None, op1=None):
        if op1 is None:
            r.add(eng, lambda e: e.tensor_scalar(out=out, in0=in0, scalar1=s1, scalar2=None, op0=op0),
                  reads=reads, writes=writes)
        else:
            r.add(eng, lambda e: e.tensor_scalar(out=out, in0=in0, scalar1=s1, scalar2=s2, op0=op0, op1=op1),
                  reads=reads, writes=writes)

    def stt(out, in0, scalar, in1, op0, op1, reads, writes):
        r.add(DVE, lambda e: e.scalar_tensor_tensor(out=out, in0=in0, scalar=scalar, in1=in1, op0=op0, op1=op1),
              reads=reads, writes=writes)

    def cp(eng, out, in_, reads, writes):
        r.add(eng, lambda e: e.tensor_copy(out=out, in_=in_), reads=reads, writes=writes)

    def memset(eng, ap, val, writes):
        r.add(eng, lambda e: e.memset(ap, val), writes=writes)

    def transpose(out, in_, idn, reads, writes):
        r.add(PE, lambda e: e.transpose(out=out, in_=in_, identity=idn), reads=reads, writes=writes)

    def allgather(src, dst, reads, writes):
        r.add(POOL, lambda e: e.collective_compute(
            "AllGather", ALU.bypass, replica_groups=[list(range(g * R, (g + 1) * R)) for g in range(cfg["NG"])],
            ins=[src.ap().opt()], outs=[dst.ap().opt()]), reads=reads, writes=writes, dma="cc")

    dma(SP, xT[:], xT_in.rearrange("(k p) t -> p k t", p=128), [], ["xT"])
    dma(SP, identS[:], ident, [], ["identS"])
    dma(SP, antiS[:], antiI, [], ["antiS"])
    dma(SP, lngS[:], lng.rearrange("l i p k -> p (l i) k"), [], ["lngS"], nonc=True)
    dma(SP, lnbS[:], lnb.rearrange("l i p k -> p (l i) k"), [], ["lnbS"], nonc=True)
    dma(SP, convwS[:], convw.rearrange("l p c j -> p l (c j)"), [], ["convwS"], nonc=True)
    dma(SP, rselS[:], rsel, [], ["rselS"])
    dma(SP, hselS[:], hsel, [], ["hselS"])
    dma(SP, iotaS[:], iotap, [], ["iotaS"])
    dma(SP, ptB[:], ptab.partition_broadcast(128), [], ["ptB"])
    dma(SP, selS[:], sel, [], ["selS"])
    dma(SP, tabS[0:32, :], tab, [], ["tabS"])
    dma(SP, tabo[0:32, :], tabown, [], ["tabo"])
    dma(SP, b31[:], tab[31:32, :].partition_broadcast(128), [], ["b31"])
    dma(SP, b31o[:], tabown[31:32, :].partition_broadcast(128), [], ["b31o"])
    memset(DVE, onesm[:], 1.0 / 1024.0, ["onesm"])
    memset(DVE, tabS[32:33, :], NEG, ["tabS"])
    memset(DVE, tabo[32:33, :], NEG, ["tabo"])

    for x0 in range(0, WLEN, 512):
        xw = min(512, WLEN - x0)
        mm(ps[0][0:4, 0:xw], tabS[:, :], selS[:, x0:x0 + xw], True, True, ["tabS", "selS"], ["ps0"])
        cp(DVE, Fsb[:, x0:x0 + xw], ps[0][0:4, 0:xw], ["ps0"], ["Fsb"])
    dma(SP, fscr[:, :], Fsb[:], ["Fsb"], ["fscr"])
    dma(SP, selS[:, 0:512], sels, ["selS", "Fsb"], ["selS2"])
    mm(ps[1][0:1, 0:512], tabo[:, :], selS[:, 0:512], True, True, ["tabo", "selS2"], ["ps1"])
    cp(DVE, Fsb[0:1, 0:512], ps[1][0:1, 0:512], ["ps1", "fscr"], ["Fsb2"])
    dma(SP, fscr_s[:, :], Fsb[0:1, 0:512], ["Fsb2"], ["fscr_s"])
    for which, off in ((BL, 0), (Bn, 256)):
        dma(SP, HkS[:, 0:T], bass.AP(fscr_s.ap().tensor, off, [[1, 128], [1, T]]),
            ["fscr_s", "psS"], ["HkS"])
        mm(ps[2][:, 0:T], antiS[:], HkS[:, 0:T], True, True, ["antiS", "HkS"], ["ps2"])
        cp(DVE, which[:, 0:T], ps[2][:, 0:T], ["ps2"], ["psS", "Bsm"])
        cp(DVE, which[:, T:2 * T], ps[2][:, 0:T], ["ps2"], ["psS", "Bsm"])

    cp(DVE, idxf[:, :], ptB[:, :], ["ptB"], ["idxf"])
    ts(DVE, idxf[:, :], idxf[:, :], 128.0, ALU.mult, ["idxf", "iotaS"], ["idxf"], s2=iotaS[:, 0:1], op1=ALU.add)
    cp(DVE, IDX[:, :], idxf[:, :], ["idxf"], ["IDX"])
    r.barrier()
    def make_xb(c0, w):
        act(xb[:, :, 0:w], xT[:, :, c0:c0 + w], AF.Copy, ["xT"], ["xb"])

    def layer_norm(l, i, c0, w):
        rb = hT[:, 0:KC, 0:w]
        rsq = hT[:, KC:2 * KC, 0:w]
        act(rb, xT[:, :, c0:c0 + w], AF.Copy, ["xT"], ["hT"])
        act(rsq, xT[:, :, c0:c0 + w], AF.Square, ["xT"], ["hT"])
        rsq_f = lambda k: hT[:, KC + k, 0:w]
        rtok = "hT"
        for k in range(KC):
            mm(ps[0][:, 0:w], onesm[:], hT[:, k, 0:w], k == 0, k == KC - 1, ["onesm", "hT"], ["ps0"])
        for k in range(KC):
            mm(ps[1][:, 0:w], onesm[:], rsq_f(k), k == 0, k == KC - 1, ["onesm", rtok], ["ps1"])
        mean, var, rstd = stat[0][:, 0:w], stat[1][:, 0:w], stat[2][:, 0:w]
        cp(DVE, mean, ps[0][:, 0:w], ["ps0"], ["stat0"])
        tt(DVE, var, mean, mean, ALU.mult, ["stat0"], ["stat1"])
        tt(DVE, var, ps[1][:, 0:w], var, ALU.subtract, ["ps1", "stat1"], ["stat1"])
        ts(DVE, var, var, LN_EPS, ALU.add, ["stat1"], ["stat1"])
        act(var, var, AF.Sqrt, ["stat1"], ["stat1"])
        r.add(DVE, lambda e, rstd=rstd, var=var: e.reciprocal(out=rstd, in_=var), reads=["stat1"], writes=["stat2"])
        gi = (l * 3 + i) * KC
        for k in range(KC):
            xs = xT[:, k, c0:c0 + w]
            tt(DVE, xs, xs, mean, ALU.subtract, ["xT", "stat0"], ["xT"])
            tt(DVE, xs, xs, rstd, ALU.mult, ["xT", "stat2"], ["xT"])
            ts(DVE, xs, xs, lngS[:, gi + k:gi + k + 1], ALU.mult, ["xT", "lngS", "lnbS"], ["xT"],
               s2=lnbS[:, gi + k:gi + k + 1], op1=ALU.add)
        make_xb(c0, w)

    wcnt = {"g": 0, "d": 0, "i": 0, "s": 0, "kv": 0}
    wgb = [wbig[i][:, :, 0:256] for i in range(2)]
    wub = [wbig[i][:, :, 256:512] for i in range(2)]
    winb = wbig

    def ffn(l, which, c0, w):
        wgl = wg[which][l].rearrange("(k p) f -> p k f", p=128)
        wul = wu[which][l].rearrange("(k p) f -> p k f", p=128)
        wdl = wd[which][l].rearrange("(f p) o -> p f o", p=128)
        for fb in range(NFB):
            bi = wcnt["i"] % 2
            wcnt["i"] += 1
            dma(POOL, wgb[bi][:], wgl[:, :, fb * FB:(fb + 1) * FB], [], [f"winb{bi}"])
            dma(POOL, wub[bi][:], wul[:, :, fb * FB:(fb + 1) * FB], [], [f"winb{bi}"])
            for fc in range(FB // 128):
                f = fb * (FB // 128) + fc
                pg, pu = ps[f % 2], ps[2 + f % 2]
                for k in range(KC):
                    mm(pg[:, 0:w], wgb[bi][:, k, fc * 128:(fc + 1) * 128], xb[:, k, 0:w], k == 0, k == KC - 1,
                       [f"winb{bi}", "xb"], [f"ps{f % 2}"])
                for k in range(KC):
                    mm(pu[:, 0:w], wub[bi][:, k, fc * 128:(fc + 1) * 128], xb[:, k, 0:w], k == 0, k == KC - 1,
                       [f"winb{bi}", "xb"], [f"ps{2 + f % 2}"])
                si = wcnt["s"] % 2
                wcnt["s"] += 1
                act(sgt[si][:, 0:w], pg[:, 0:w], AF.Silu, [f"ps{f % 2}"], [f"sgt{si}"])
                tt(DVE, hT[:, f, 0:w], sgt[si][:, 0:w], pu[:, 0:w], ALU.mult, [f"sgt{si}", f"ps{2 + f % 2}"], ["hT"])
        ckpt(0.2)
        for half in range(2):
            for f2 in range(0, FCH, 2):
                bi = wcnt["d"] % 3
                wcnt["d"] += 1
                nf = min(2, FCH - f2)
                dma(POOL, wdb[bi][:, 0:nf, :], wdl[:, f2:f2 + nf, half * 512:(half + 1) * 512], [], [f"wdb{bi}"])
                for ff in range(nf):
                    f = f2 + ff
                    for o in range(4):
                        mm(ps[4 + o][:, 0:w], wdb[bi][:, ff, o * 128:(o + 1) * 128], hT[:, f, 0:w],
                           f == 0, f == FCH - 1, [f"wdb{bi}", "hT"], [f"ps{4 + o}"])
            for o in range(4):
                oc = half * 4 + o
                xs = xT[:, oc, c0:c0 + w]
                ts(DVE, xs, xs, ALPHA, ALU.mult, ["xT"], ["xT"])
                stt(xs, ps[4 + o][:, 0:w], 0.5, xs, ALU.mult, ALU.add, [f"ps{4 + o}", "xT"], ["xT"])

    STOP = cfg.get('STOP', 99)

    def ckpt(level):
        if STOP <= level:
            raise StopBuild()

    try:
      for l in range(DEPTH):
          if l == 0:
              ckpt(0)
          lam_init = 0.8 - 0.6 * math.exp(-0.3 * l)
          dma(SP, lamt[:, :], lamp[l:l + 1, :].partition_broadcast(128), ["lamS"], ["lamt"])
          tt(DVE, lamt[:, 0:64], lamt[:, 0:64], lamt[:, 64:128], ALU.mult, ["lamt"], ["lamt"])
          tt(DVE, lamt[:, 128:192], lamt[:, 128:192], lamt[:, 192:256], ALU.mult, ["lamt"], ["lamt"])
          r.add(DVE, lambda e: e.reduce_sum(out=lamS[:, 0:1], in_=lamt[:, 0:64], axis=mybir.AxisListType.X),
                reads=["lamt"], writes=["lamS"])
          r.add(DVE, lambda e: e.reduce_sum(out=lamS[:, 1:2], in_=lamt[:, 128:192], axis=mybir.AxisListType.X),
                reads=["lamt"], writes=["lamS"])
          act(lamS[:, 2:4], lamS[:, 0:2], AF.Exp, ["lamS"], ["lamS"])
          tt(DVE, lamS[:, 4:5], lamS[:, 2:3], lamS[:, 3:4], ALU.subtract, ["lamS"], ["lamS"])
          ts(DVE, lamS[:, 4:5], lamS[:, 4:5], lam_init, ALU.add, ["lamS"], ["lamS"])
          dma(SP, swS[:, :], subw[l:l + 1, :].partition_broadcast(128), ["swS"], ["swS"])
          ts(DVE, swS[:, :], swS[:, :], 1.0 - lam_init, ALU.mult, ["swS"], ["swS"])
          lamcol = lamS[:, 4:5]

          winl = w_in[l].rearrange("(k p) f -> p k f", p=128)
          for gi, (c0, w) in enumerate(groups):
              make_xb(c0, w)
              ckpt(0.1 + 10 * l)
              ffn(l, 0, c0, w)
              ckpt(0.4 + 10 * l)
              layer_norm(l, 0, c0, w)
              ckpt(0.6 + 10 * l)
              is_s = gi == NS
              for blk in range(6):
                  bi = wcnt["i"] % 2
                  wcnt["i"] += 1
                  dma(POOL, winb[bi][:], winl[:, :, blk * 512:(blk + 1) * 512], [], [f"winb{bi}"])
                  wb = winb[bi]
                  if blk <= 4:
                      for ch in range(4):
                          pp = ps[ch % 4]
                          for k in range(KC):
                              mm(pp[:, 0:w], wb[:, k, ch * 128:(ch + 1) * 128], xb[:, k, 0:w], k == 0, k == KC - 1,
                                 [f"winb{bi}", "xb"], [f"ps{ch % 4}"])
                          pt_ = f"ps{ch % 4}"
                          if blk == 0:
                              act(st4[:, ch, 0:w], pp[:, 0:w], AF.Copy, [pt_], ["st4"])
                          elif blk == 1:
                              act(cgt[:, ch, 0:w], pp[:, 0:w], AF.Copy, [pt_], ["cgt"])
                          elif blk == 2:
                              tt(DVE, st4[:, ch, 0:w], pp[:, 0:w], cgt[:, ch, 0:w], ALU.mult, [pt_, "cgt"], ["st4"])
                          elif blk == 3:
                              act(stb[:, ch, 0:w], pp[:, 0:w], AF.Identity, [pt_], ["stb"], scale=0.125)
                          else:
                              act(stb[:, ch, 0:w], pp[:, 0:w], AF.Copy, [pt_], ["stb"])
                      if blk == 0:
                          dma(SP, bgs[:, :, c0:c0 + w].rearrange("c p t -> p c t"), st4[:, :, 0:w], ["st4"], ["bgs"])
                      elif blk == 2:
                          dma(SP, us[:, :, c0:c0 + w].rearrange("c p t -> p c t"), st4[:, :, 0:w], ["st4"], ["us"])
                          if not is_s:
                              dma(SP, xt[l][gi:gi + 1, :].rearrange("o (c p two) -> p (o c) two", p=128, two=2),
                                  st4[:, :, w - 2:w], ["st4"], [f"xt{l}"], nonc=True)
                              dma(SP, convp[l, gi].rearrange("(c p) two -> p c two", p=128),
                                  st4[:, :, w - 2:w], ["st4"], [], nonc=True)
                          else:
                              for cc in range(4):
                                  dma(SP, convs[l, cc * 128:(cc + 1) * 128, :, :],
                                      st4[:, cc, 0:w].rearrange("p (b t) -> p b t", t=T)[:, :, T - 2:T],
                                      ["st4"], [], nonc=True)
                      elif blk == 3:
                          if not is_s:
                              dma(SP, qs[:, :, c0:c0 + w].rearrange("h p t -> p h t"), stb[:, :, 0:w], ["stb"], ["qs"])
                          else:
                              dma(SP, xk[l][NS][:, NSMP:2 * NSMP].rearrange("(h p) t -> p h t", p=128),
                                  stb[:, :, 0:w], ["stb"], [f"xk{l}_{NS}"])
                      elif blk == 4:
                          dma(SP, xk[l][gi][:, 0:w].rearrange("(h p) t -> p h t", p=128),
                              stb[:, :, 0:w], ["stb"], [f"xk{l}_{gi}"])
                  ckpt(0.70 + 0.01 * blk)
                  if blk >= 4:
                      dst = ko if blk == 4 else vo
                      for t0 in range(0, w, 128):
                          tw = min(128, w - t0)
                          pp = ps[4 + (wcnt["kv"] % 4)]
                          ptk = f"ps{4 + (wcnt['kv'] % 4)}"
                          si = wcnt["kv"] % 2
                          wcnt["kv"] += 1
                          DBG = cfg.get("DBG", "")
                          if "nomm" not in DBG:
                              if "n256" in DBG:
                                  for hf in range(2):
                                      for k in range(KC):
                                          mm(pp[0:tw, hf * 256:(hf + 1) * 256], xb[:, k, t0:t0 + tw],
                                             wb[:, k, hf * 256:(hf + 1) * 256], k == 0 and hf == 0,
                                             k == KC - 1 and hf == 1, [f"winb{bi}", "xb"], [ptk])
                              else:
                                  for k in range(KC):
                                      mm(pp[0:tw, :], xb[:, k, t0:t0 + tw], wb[:, k, :], k == 0, k == KC - 1,
                                         [f"winb{bi}", "xb"], [ptk])
                          if "nocp" not in DBG:
                              act(kvst[si][0:tw, :], pp[0:tw, :], AF.Copy, [ptk], [f"kvst{si}"])
                          if "nodma" not in DBG:
                              dma(SP, dst[l, c0 + t0:c0 + t0 + tw, :], kvst[si][0:tw, :], [f"kvst{si}"], [])
                          if blk == 5:
                              act(kvb[si][0:tw, :], pp[0:tw, :], AF.Copy, [ptk], [f"kvb{si}"])
                              dma(SP, xv[l][gi][t0:t0 + tw, :], kvb[si][0:tw, :], [f"kvb{si}"], [f"xv{l}_{gi}"])
                  ckpt(0.75 + 0.002 * (blk - 4) + 0.01 * gi if blk >= 4 else -1)

          ckpt(1 + 10 * l)
          for s_ in range(NS + 1):
              allgather(xk[l][s_], xk_all[l][s_], [f"xk{l}_{s_}"], [f"xk_all{l}_{s_}"])
              allgather(xv[l][s_], xv_all[l][s_], [f"xv{l}_{s_}"], [f"xv_all{l}_{s_}"])
          allgather(xt[l], xt_all[l], [f"xt{l}"], [f"xt_all{l}"])

          ckpt(2 + 10 * l)
          r.barrier()
          memset(DVE, Vh[:, :, 128:130], 1.0, ["Vh"])
          memset(DVE, Vs[:, :, 128:130], 1.0, ["Vs"])
          memset(DVE, Qbd[:, :, :], 0.0, ["Qbd"])
          for h in range(4):
              if True:
                  for x0 in range(0, WW, 512):
                      xw = min(512, WW - x0)
                      dma(SP, Hk[:, 0:xw], bass.AP(fscr.ap().tensor, h * WLEN + x0, [[1, 128], [1, xw]]), ["fscr"], ["Hk"])
                      mm(ps[0][:, 0:xw], antiS[:], Hk[:, 0:xw], True, True, ["antiS", "Hk"], ["ps0"])
                      cp(DVE, Wwin[:, x0:x0 + xw], ps[0][:, 0:xw], ["ps0"], ["Wwin"])
              for rr in range(R):
                  for s in range(NS):
                      k0 = (s * R + rr) * C
                      dma(SP, KTh[:, k0:k0 + C],
                          xk_all[l][s][rr * 512 + h * 128:rr * 512 + (h + 1) * 128, 0:C],
                          [f"xk_all{l}_{s}"], ["KTh"])
                      kt0 = (s * R + rr) * TPC
                      dma(SP, Vh[:, kt0:kt0 + TPC, 0:128],
                          xv_all[l][s][rr * C:(rr + 1) * C, h * 128:(h + 1) * 128]
                          .rearrange("(w p) d -> p w d", p=128), [f"xv_all{l}_{s}"], ["Vh"])
              dma(SP, QTh[:, :], qs[h, :, :], ["qs"], ["QTh"])
              ucount = 0
              for s in range(NS):
                  nkt = NT * (s + 1)
                  for kt in range(nkt):
                      near = kt >= NT * s - 1
                      for c in range(2):
                          sp_i = (ucount % 2) * 2 + c
                          mm(ps[sp_i][:, 0:C], KTh[64 * c:64 * c + 64, kt * 128:(kt + 1) * 128],
                             QTh[64 * c:64 * c + 64, s * C:(s + 1) * C], True, True, ["KTh", "QTh"], [f"ps{sp_i}"])
                          if near:
                              t = kt - NT * s
                              off = 128 * (NT - 1 - t)
                              tt(DVE, Stmp[c][:, :], ps[sp_i][:, 0:C], Wwin[:, off:off + C], ALU.add,
                                 [f"ps{sp_i}", "Wwin"], [f"Stmp{c}"])
                              act(Pt[sp_i][:, :], Stmp[c][:, :], AF.Exp, [f"Stmp{c}"], [f"Pt{sp_i}"])
                          else:
                              act(Pt[sp_i][:, :], ps[sp_i][:, 0:C], AF.Exp, [f"ps{sp_i}", "b31"], [f"Pt{sp_i}"],
                                  bias=b31[:, h:h + 1])
                      for c in range(2):
                          sp_i = (ucount % 2) * 2 + c
                          for qb in range(TPC):
                              a = qb * 2 + c
                              ob = 4 + a // 2
                              oo = (a % 2) * 256
                              mm(ps[ob][:, oo:oo + 129], Pt[sp_i][:, qb * 128:(qb + 1) * 128], Vh[:, kt, 0:129],
                                 kt == 0 and c == 0, kt == nkt - 1 and c == 1, [f"Pt{sp_i}", "Vh"], [f"ps{ob}"])
                      ucount += 1
                  for qb in range(TPC):
                      o0 = ps[4 + (qb * 2) // 2][:, 0:129]
                      o1 = ps[4 + (qb * 2 + 1) // 2][:, 256:256 + 129]
                      pt0, pt1 = f"ps{4 + qb}", f"ps{4 + qb}"
                      r.add(DVE, lambda e, o0=o0: e.reciprocal(out=finc[:, 0:1], in_=o0[:, 128:129]),
                            reads=[pt0], writes=["finc"])
                      r.add(DVE, lambda e, o1=o1: e.reciprocal(out=finc[:, 1:2], in_=o1[:, 128:129]),
                            reads=[pt1], writes=["finc"])
                      tt(DVE, finc[:, 1:2], finc[:, 1:2], lamcol, ALU.mult, ["finc", "lamS"], ["finc"])
                      ts(DVE, fin[0][:, :], o1[:, 0:128], finc[:, 1:2], ALU.mult, [pt1, "finc"], ["fin0"])
                      stt(fin[1][:, :], o0[:, 0:128], finc[:, 0:1], fin[0][:, :], ALU.mult, ALU.subtract,
                          [pt0, "finc", "fin0"], ["fin1"])
                      act(fin[0][:, :], fin[1][:, :], AF.Square, ["fin1"], ["fin0", "finc2"], accum=finc[:, 2:3])
                      ts(DVE, finc[:, 3:4], finc[:, 2:3], 1.0 / 128.0, ALU.mult, ["finc2"], ["finc3"], s2=LN_EPS, op1=ALU.add)
                      act(finc[:, 3:4], finc[:, 3:4], AF.Sqrt, ["finc3"], ["finc3"])
                      r.add(DVE, lambda e: e.reciprocal(out=finc[:, 3:4], in_=finc[:, 3:4]), reads=["finc3"], writes=["finc3"])
                      stt(fin[2][:, :], fin[1][:, :], finc[:, 3:4], swS[:, :], ALU.mult, ALU.mult,
                          ["fin1", "finc3", "swS"], ["fin2"])
                      transpose(ps[0][:, 0:128], fin[2][:, :], identS[:], ["fin2", "identS"], ["ps0"])
                      col = s * C + qb * 128
                      act(zTa[:, h, col:col + 128], ps[0][:, 0:128], AF.Copy, ["ps0"], ["zTa"])

          ckpt(3 + 10 * l)
          for hh in range(4):
              dma(SP, SQt[hh % 2][:, :, :],
                  xk_all[l][NS].ap().rearrange("(r h p) t -> h p r t", r=R, h=4)[hh, :, :, :],
                  [f"xk_all{l}_{NS}"], [f"SQt{hh % 2}"])
              if hh == 0:
                  ts(DVE, Qsel[:, :, :], SQt[0][:, :, :], rselS[:, 0:1], ALU.mult, ["SQt0", "rselS"], ["Qsel"])
              else:
                  r.add(DVE, lambda e, hh=hh: e.scalar_tensor_tensor(
                      out=Qsel[:, :, :], in0=SQt[hh % 2][:, :, :], scalar=rselS[:, hh:hh + 1], in1=Qsel[:, :, :],
                      op0=ALU.mult, op1=ALU.add), reads=[f"SQt{hh % 2}", "rselS", "Qsel"], writes=["Qsel"])
          cp(DVE, KnT[:, :].rearrange("p (r t) -> p r t", r=R), Qsel[:, :, 0:NSMP], ["Qsel"], ["KnT"])
          for c in range(2):
              for rr in range(R):
                  cp(DVE, Qbd[64 * c:64 * c + 64, rr * BPC:(rr + 1) * BPC, c * T:(c + 1) * T],
                     Qsel[64 * c:64 * c + 64, rr, NSMP:2 * NSMP].rearrange("p (b t) -> p b t", t=T), ["Qsel"], ["Qbd"])
          for b in range(NB):
              rb_, bb = b // BPC, b % BPC
              bi = b % 2
              for j in range(NPG):
                  col = b * NPG + j
                  r.add(POOL, lambda e, bi=bi, j=j, col=col, l=l: e.indirect_dma_start(
                      out=Kraw[bi][:, j, :], out_offset=None, in_=ck[l],
                      in_offset=bass.IndirectOffsetOnAxis(ap=IDX[:, col:col + 1], axis=0)),
                      reads=["IDX"], writes=[f"Kraw{bi}"], dma=True)
                  r.add(POOL, lambda e, j=j, col=col, l=l: e.indirect_dma_start(
                      out=Vraw[0][:, j, :], out_offset=None, in_=cv[l],
                      in_offset=bass.IndirectOffsetOnAxis(ap=IDX[:, col:col + 1], axis=0)),
                      reads=["IDX"], writes=["Vraw0"], dma=True)
              vi = b % 2
              row0 = rb_ * NSMP + bb * T
              dma(SP, Vn4[vi][:, :, :], xv_all[l][NS][row0:row0 + T, :].rearrange("n (h d) -> n h d", h=4),
                  [f"xv_all{l}_{NS}"], [f"Vn4{vi}"])
              ts(DVE, Vs[0:T, NPG, 0:128], Vn4[vi][:, 0, :], rselS[0:T, 0:1], ALU.mult, [f"Vn4{vi}", "rselS"], ["Vs"])
              for hh in range(1, 4):
                  r.add(DVE, lambda e, hh=hh, vi=vi: e.scalar_tensor_tensor(
                      out=Vs[0:T, NPG, 0:128], in0=Vn4[vi][:, hh, :], scalar=rselS[0:T, hh:hh + 1], in1=Vs[0:T, NPG, 0:128],
                      op0=ALU.mult, op1=ALU.add), reads=[f"Vn4{vi}", "rselS", "Vs"], writes=["Vs"])
              ckpt(3.1 + 10 * l)
              cp(POOL, Vs[:, 0:NPG, 0:128], Vraw[0][:, :, :], ["Vraw0"], ["Vs"])
              ckpt(3.2 + 10 * l)
              for j0 in range(0, NPG, 4):
                  nj = min(4, NPG - j0)
                  pb = ps[(j0 // 4) % 2]
                  for jj in range(nj):
                      transpose(pb[:, jj * 128:(jj + 1) * 128], Kraw[bi][:, j0 + jj, :], identS[:],
                                [f"Kraw{bi}", "identS"], [f"ps{(j0 // 4) % 2}"])
                  act(KTs[:, j0 * 128:(j0 + nj) * 128], pb[:, 0:nj * 128], AF.Copy, [f"ps{(j0 // 4) % 2}"], ["KTs"])
              ckpt(3.3 + 10 * l)
              SS = ps[2]
              for j in range(NPG):
                  mm(SS[:, j * 16:(j + 1) * 16], KTs[:, j * 128:(j + 1) * 128], Qbd[:, b, :], True, True,
                     ["KTs", "Qbd"], ["ps2"])
              mm(SS[0:T, NPG * 16:(NPG + 1) * 16], KnT[:, b * T:(b + 1) * T], Qbd[:, b, :], True, True,
                 ["KnT", "Qbd"], ["ps2"])
              ckpt(3.4 + 10 * l)
              DBG = cfg.get("DBG", "")
              ser = ["ssser"] if "ser" in DBG else []
              if NPG > 1:
                  act(Ps[:, 0:(NPG - 1) * 16], SS[:, 0:(NPG - 1) * 16], AF.Exp, ["ps2", "b31o"], ["Ps"] + ser, bias=b31o[:, 0:1])
              if "actev" in DBG:
                  act(Stmp[1][:, 0:32], SS[:, (NPG - 1) * 16:(NPG + 1) * 16], AF.Copy, ["ps2"], ["Stmp1"] + ser)
                  tt(DVE, Stmp[0][:, 0:16], Stmp[1][:, 0:16], BL[:, :], ALU.add, ["Stmp1", "Bsm"], ["Stmp0"])
                  tt(DVE, Stmp[0][0:T, 16:32], Stmp[1][0:T, 16:32], Bn[0:T, :], ALU.add, ["Stmp1", "Bsm"], ["Stmp0"])
              else:
                  tt(DVE, Stmp[0][:, 0:16], SS[:, (NPG - 1) * 16:NPG * 16], BL[:, :], ALU.add, ["ps2", "Bsm"], ["Stmp0"] + ser)
                  tt(DVE, Stmp[0][0:T, 16:32], SS[0:T, NPG * 16:(NPG + 1) * 16], Bn[0:T, :], ALU.add, ["ps2", "Bsm"], ["Stmp0"] + ser)
              act(Ps[:, (NPG - 1) * 16:NPG * 16], Stmp[0][:, 0:16], AF.Exp, ["Stmp0"], ["Ps"])
              act(Ps[0:T, NPG * 16:(NPG + 1) * 16], Stmp[0][0:T, 16:32], AF.Exp, ["Stmp0"], ["Ps"])
              ckpt(3.5 + 10 * l)
              OS = ps[3]
              for c in range(2):
                  for j in range(NPG):
                      mm(OS[0:T, c * 256:c * 256 + 129], Ps[:, j * 16 + c * T:j * 16 + (c + 1) * T], Vs[:, j, 0:129],
                         j == 0, False, ["Ps", "Vs"], ["ps3"])
                  mm(OS[0:T, c * 256:c * 256 + 129], Ps[0:T, NPG * 16 + c * T:NPG * 16 + (c + 1) * T], Vs[0:T, NPG, 0:129],
                     False, True, ["Ps", "Vs"], ["ps3"])
              ckpt(3.6 + 10 * l)
              o0 = OS[0:T, 0:129]
              o1 = OS[0:T, 256:256 + 129]
              r.add(DVE, lambda e, o0=o0: e.reciprocal(out=finc[0:T, 0:1], in_=o0[:, 128:129]), reads=["ps3"], writes=["finc"])
              r.add(DVE, lambda e, o1=o1: e.reciprocal(out=finc[0:T, 1:2], in_=o1[:, 128:129]), reads=["ps3"], writes=["finc"])
              tt(DVE, finc[0:T, 1:2], finc[0:T, 1:2], lamcol[0:T, :], ALU.mult, ["finc", "lamS"], ["finc"])
              ts(DVE, fin[0][0:T, :], o1[:, 0:128], finc[0:T, 1:2], ALU.mult, ["ps3", "finc"], ["fin0"])
              stt(fin[1][0:T, :], o0[:, 0:128], finc[0:T, 0:1], fin[0][0:T, :], ALU.mult, ALU.subtract,
                  ["ps3", "finc", "fin0"], ["fin1"])
              act(fin[0][0:T, :], fin[1][0:T, :], AF.Square, ["fin1"], ["fin0", "finc2"], accum=finc[0:T, 2:3])
              ts(DVE, finc[0:T, 3:4], finc[0:T, 2:3], 1.0 / 128.0, ALU.mult, ["finc2"], ["finc3"], s2=LN_EPS, op1=ALU.add)
              act(finc[0:T, 3:4], finc[0:T, 3:4], AF.Sqrt, ["finc3"], ["finc3"])
              r.add(DVE, lambda e: e.reciprocal(out=finc[0:T, 3:4], in_=finc[0:T, 3:4]), reads=["finc3"], writes=["finc3"])
              stt(fin[2][0:T, :], fin[1][0:T, :], finc[0:T, 3:4], swS[0:T, :], ALU.mult, ALU.mult,
                  ["fin1", "finc3", "swS"], ["fin2"])
              transpose(ps[4][:, b * T:(b + 1) * T], fin[2][0:T, :], identS[0:T, 0:T], ["fin2", "identS"], ["ps4"])
          act(zsT[:, :], ps[4][:, 0:NB * T], AF.Copy, ["ps4"], ["zsT"])
          dma(SP, xz[l][:, :], zsT[:, :], ["zsT"], [f"xz{l}"])
          allgather(xz[l], xz_all[l], [f"xz{l}"], [f"xz_all{l}"])
          dma(SP, zall[:, :, :, :].rearrange("p h r t -> p h (r t)"),
              xz_all[l].ap().rearrange("(h p) t -> p h t", p=128), [f"xz_all{l}"], ["zall"])
          ts(DVE, zTa[:, :, NTP:NTOK], zall[:, :, 0, :], rselS[:, 0:1], ALU.mult, ["zall", "rselS"], ["zTa"])
          for rr in range(1, R):
              r.add(DVE, lambda e, rr=rr: e.scalar_tensor_tensor(
                  out=zTa[:, :, NTP:NTOK], in0=zall[:, :, rr, :], scalar=rselS[:, rr:rr + 1], in1=zTa[:, :, NTP:NTOK],
                  op0=ALU.mult, op1=ALU.add), reads=["zall", "rselS", "zTa"], writes=["zTa"])
          for rw in range(R * NS):
              dma(SP, xtl[:, rw, :].rearrange("p (c two) -> p c two", two=2),
                  xt_all[l][rw:rw + 1, :].rearrange("o (c p two) -> p (o c) two", p=128, two=2),
                  [f"xt_all{l}"], ["xtl"], nonc=True)

          ckpt(4 + 10 * l)
          r.barrier()
          woutl = w_out[l].rearrange("(k p) f -> p k f", p=128)
          for gi, (c0, w) in enumerate(groups):
              is_s = gi == NS
              nb_ = BPC if is_s else 1
              tl = T if is_s else w
              U = ubuf[:, :, 0:nb_ * (tl + 2)].rearrange("p c (b t) -> p c b t", b=nb_)
              for cc in range(4):
                  dma(SP, U[:, cc, :, 2:tl + 2], us[cc, :, c0:c0 + w].rearrange("p (b t) -> p b t", b=nb_), ["us"], ["ubuf"],
                      nonc=is_s)
              dma(SP, st4[:, :, 0:w], bgs[:, :, c0:c0 + w].rearrange("c p t -> p c t"), ["bgs"], ["st4"])
              if is_s:
                  for cc in range(4):
                      dma(SP, U[:, cc, :, 0:2], sconv[l, :, cc, :, :], [], ["ubuf"], nonc=True)
              else:
                  hv = halo[:, :, :].rearrange("p c two -> p (c two)")
                  ts(DVE, hv, xtl[:, 0, :], hselS[:, gi * R * NS:gi * R * NS + 1], ALU.mult, ["xtl", "hselS"], ["halo"])
                  for rw in range(1, R * NS):
                      r.add(DVE, lambda e, rw=rw, hv=hv, gi=gi: e.scalar_tensor_tensor(
                          out=hv, in0=xtl[:, rw, :], scalar=hselS[:, gi * R * NS + rw:gi * R * NS + rw + 1], in1=hv,
                          op0=ALU.mult, op1=ALU.add), reads=["xtl", "hselS", "halo"], writes=["halo"])
                  cp(DVE, U[:, :, 0, 0:2], halo[:, :, :], ["halo"], ["ubuf"])
              for cc in range(4):
                  wi = l * 12 + cc * 3
                  acc = cgt[:, cc, 0:w].rearrange("p (b t) -> p b t", b=nb_)
                  ts(DVE, acc, U[:, cc, :, 0:tl], convwS[:, wi:wi + 1], ALU.mult, ["ubuf", "convwS"], ["cgt"])
                  for jx in (1, 2):
                      r.add(DVE, lambda e, acc=acc, cc=cc, jx=jx, wi=wi, U=U, tl=tl: e.scalar_tensor_tensor(
                          out=acc, in0=U[:, cc, :, jx:jx + tl], scalar=convwS[:, wi + jx:wi + jx + 1], in1=acc,
                          op0=ALU.mult, op1=ALU.add), reads=["ubuf", "convwS", "cgt"], writes=["cgt"])
                  tt(DVE, zc[:, cc, 0:w], cgt[:, cc, 0:w], st4[:, cc, 0:w], ALU.mult, ["cgt", "st4"], ["zc"])
              for half in range(2):
                  bi = wcnt["i"] % 2
                  wcnt["i"] += 1
                  dma(POOL, winb[bi][:], woutl[:, :, half * 512:(half + 1) * 512], [], [f"winb{bi}"])
                  for o in range(4):
                      oc = half * 4 + o
                      for k in range(KC):
                          zr = zc[:, k, 0:w] if k < 4 else zTa[:, k - 4, c0:c0 + w]
                          mm(ps[4 + o][:, 0:w], winb[bi][:, k, o * 128:(o + 1) * 128], zr,
                             k == 0, k == KC - 1, [f"winb{bi}", "zc", "zTa"], [f"ps{4 + o}"])
                      xs = xT[:, oc, c0:c0 + w]
                      stt(xs, xs, ALPHA, ps[4 + o][:, 0:w], ALU.mult, ALU.add, [f"ps{4 + o}", "xT"], ["xT"])
              layer_norm(l, 1, c0, w)
              ffn(l, 1, c0, w)
              layer_norm(l, 2, c0, w)
              if l == DEPTH - 1:
                  dma(SP, yT.rearrange("(k p) t -> p k t", p=128)[:, :, c0:c0 + w], xT[:, :, c0:c0 + w], ["xT"], [])

    except StopBuild:
        pass
    r.emit()
    return nc


def _bucket(n):
    n = np.asarray(n)
    nn = np.maximum(n, 0)
    nf = np.maximum(nn, 1).astype(np.float32)
    large = 16 + (np.log(nf / np.float32(16)) / np.float32(math.log(128 / 16)) * np.float32(16)).astype(np.int32)
    large = np.minimum(large, 31)
    return np.where(nn < 16, nn, large)


def _sel_rows(dist):
    dist = np.asarray(dist)
    row = np.where(dist < 0, 32, _bucket(dist))
    out = np.zeros((33, dist.shape[0]), np.float32)
    out[row, np.arange(dist.shape[0])] = 1.0
    return out


def make_inputs(cfg, inp):
    C, NS, R, NG, BPC, NPG, DFF, NPHYS = (cfg[k] for k in ("C", "NS", "R", "NG", "BPC", "NPG", "DFF", "NPHYS"))
    T = 8
    NSMP = BPC * T
    NTP = NS * C
    TPC = C // 128
    NT = R * TPC
    WW = C + 128 * NT
    WLEN = ((WW + 127 + 127) // 128) * 128
    f32 = np.float32
    xp = np.asarray(inp["x_prompt"], f32)
    xs = np.asarray(inp["x_sample"], f32)
    ckf = np.asarray(inp["cache_k"], f32)
    cvf = np.asarray(inp["cache_v"], f32)
    pt = np.asarray(inp["page_table"], np.int32)
    tab = np.ascontiguousarray(np.asarray(inp["rel_bias_table"], f32))
    common = dict(
        wg1=np.asarray(inp["ffn1_w_gate"], f32), wu1=np.asarray(inp["ffn1_w_up"], f32), wd1=np.asarray(inp["ffn1_w_down"], f32),
        wg2=np.asarray(inp["ffn2_w_gate"], f32), wu2=np.asarray(inp["ffn2_w_up"], f32), wd2=np.asarray(inp["ffn2_w_down"], f32),
        w_in=np.asarray(inp["w_in"], f32), w_out=np.asarray(inp["w_out"], f32),
        lng=np.ascontiguousarray(np.asarray(inp["ln_g"], f32).reshape(DEPTH, 3, KC, 128).transpose(0, 1, 3, 2)),
        lnb=np.ascontiguousarray(np.asarray(inp["ln_b"], f32).reshape(DEPTH, 3, KC, 128).transpose(0, 1, 3, 2)),
        convw=np.ascontiguousarray(np.asarray(inp["conv_w"], f32).reshape(DEPTH, 3, 4, 128).transpose(0, 3, 2, 1)),
        lamp=np.ascontiguousarray(np.concatenate([np.asarray(inp[k], f32) for k in
                                                  ("lambda_q1", "lambda_k1", "lambda_q2", "lambda_k2")], axis=1)),
        subw=np.asarray(inp["subln_w"], f32), tab=tab,
        antiI=np.ascontiguousarray(np.eye(128, dtype=f32)[::-1]), ident=np.eye(128, dtype=f32),
    )
    xx = np.arange(256)
    sels = np.zeros((33, 512), f32)
    sels[:, 0:256] = _sel_rows(xx + 1)
    sels[:, 256:512] = _sel_rows(xx - 127)
    common["sels"] = sels
    sc = np.asarray(inp["state_conv"], f32)
    in_maps = []
    for core in range(NG * R):
        g, j = core // R, core % R
        m = dict(common)
        cols = [xp[g, (R * s + j) * C:(R * s + j + 1) * C, :] for s in range(NS)]
        b0 = g * R * BPC + j * BPC
        cols.append(xs[b0:b0 + BPC].reshape(NSMP, D))
        m["xT_in"] = np.ascontiguousarray(np.concatenate(cols, axis=0).T)
        m["sconv"] = np.ascontiguousarray(sc[:, b0:b0 + BPC].reshape(DEPTH, BPC, 2, 4, 128).transpose(0, 4, 3, 1, 2))
        m["tabown"] = np.ascontiguousarray(tab[:, j:j + 1])
        x = np.arange(WLEN)
        m["sel"] = _sel_rows(x - 127 + C * j - 128 * (NT - 1))
        m["ptab"] = np.ascontiguousarray(pt[g * R * BPC:(g + 1) * R * BPC].reshape(1, -1))
        rs = np.zeros((128, 4), f32)
        rs[:, j] = 1.0
        m["rsel"] = rs
        hs = np.zeros((128, NS, R * NS), f32)
        for s in range(NS):
            c = R * s + j
            if c > 0:
                pc = c - 1
                hs[:, s, (pc % R) * NS + pc // R] = 1.0
        m["hsel"] = hs.reshape(128, NS * R * NS)
        m["iotap"] = np.arange(128, dtype=f32).reshape(128, 1)
        for l in range(DEPTH):
            m[f"ck{l}"] = np.ascontiguousarray(ckf[l, :, :, j, :]).reshape(NPHYS * 128, 128)
            m[f"cv{l}"] = np.ascontiguousarray(cvf[l, :, :, j, :]).reshape(NPHYS * 128, 128)
        in_maps.append(m)
    return in_maps


def assemble(cfg, res):
    C, NS, R, NG, BPC = (cfg[k] for k in ("C", "NS", "R", "NG", "BPC"))
    T = 8
    NSMP = BPC * T
    NTP = NS * C
    SEQ = R * NS * C
    f32 = np.float32
    yp = np.zeros((NG, SEQ, D), f32)
    ys = np.zeros((NG * R * BPC, T, D), f32)
    kp = np.zeros((DEPTH, NG, SEQ, 4, 128), f32)
    vp = np.zeros((DEPTH, NG, SEQ, 4, 128), f32)
    cpo = np.zeros((DEPTH, NG, 2, 512), f32)
    ks = np.zeros((DEPTH, NG * R * BPC, T, 4, 128), f32)
    vs = np.zeros((DEPTH, NG * R * BPC, T, 4, 128), f32)
    cso = np.zeros((DEPTH, NG * R * BPC, 2, 512), f32)
    for core in range(NG * R):
        g, j = core // R, core % R
        o = res[core]
        y = np.asarray(o["yT"]).T
        b0 = g * R * BPC + j * BPC
        for s in range(NS):
            c = R * s + j
            yp[g, c * C:(c + 1) * C] = y[s * C:(s + 1) * C]
            kp[:, g, c * C:(c + 1) * C] = np.asarray(o["ko"])[:, s * C:(s + 1) * C].reshape(DEPTH, C, 4, 128)
            vp[:, g, c * C:(c + 1) * C] = np.asarray(o["vo"])[:, s * C:(s + 1) * C].reshape(DEPTH, C, 4, 128)
        ys[b0:b0 + BPC] = y[NTP:].reshape(BPC, T, D)
        ks[:, b0:b0 + BPC] = np.asarray(o["ko"])[:, NTP:].reshape(DEPTH, BPC, T, 4, 128)
        vs[:, b0:b0 + BPC] = np.asarray(o["vo"])[:, NTP:].reshape(DEPTH, BPC, T, 4, 128)
        cso[:, b0:b0 + BPC] = np.asarray(o["convs"]).transpose(0, 2, 3, 1)
        if j == R - 1:
            cpo[:, g] = np.asarray(o["convp"])[:, NS - 1].transpose(0, 2, 1)
    return yp, ys, kp, vp, cpo, ks, vs, cso


_NC_CACHE = {}


def kernel(**inputs):
    cfg = FULL
    if "nc" not in _NC_CACHE:
        _NC_CACHE["nc"] = build(cfg)
    nc = _NC_CACHE["nc"]
    in_maps = make_inputs(cfg, inputs)
    res = run_bass_kernel_spmd(nc, in_maps, core_ids=list(range(cfg["NG"] * cfg["R"])))
    return assemble(cfg, res.results)
```

```python
import contextlib
import math
import numpy as np
import concourse.bass as bass
import concourse.mybir as mybir
from concourse.bass_utils import run_bass_kernel_spmd

F32 = mybir.dt.float32
BF16 = mybir.dt.bfloat16
I32 = mybir.dt.int32
AF = mybir.ActivationFunctionType
ALU = mybir.AluOpType

PE, ACT, DVE, POOL, SP = "pe", "act", "dve", "pool", "sp"
COMPUTE = (PE, ACT, DVE, POOL)
TICKS_PER_SEM = 1000000
DMA_USES_PER_SEM = 100000
NSEM_DMA = 8

D = 1024
KC = 8
DEPTH = 2
LN_EPS = 1e-5
ALPHA = (2 * DEPTH) ** 0.25
NEG = -30000.0
FULL = dict(C=512, NS=4, R=4, NG=2, BPC=16, NPG=16, DFF=2816, NPHYS=2560)


class StopBuild(Exception):
    pass


class Rec:
    def __init__(self, nc):
        self.nc = nc
        self.ops = []
        self.last_w = {}
        self.readers = {}
        self.bar = None
        self.bar_done = set()
        self.since = {}
        self.dmas_since = set()

    def barrier(self):
        self.bar = set(self.since.values()) | set(self.dmas_since)
        self.bar_done = set()
        self.dmas_since = set()

    def add(self, eng, fn, reads=(), writes=(), dma=False):
        i = len(self.ops)
        deps = set()
        psr = [t for t in reads if t.startswith("ps") and t[2:].isdigit()]
        if psr:
            writes = list(writes) + [t for t in psr if t not in writes]
        if self.bar is not None and eng not in self.bar_done:
            deps.update(self.bar)
            self.bar_done.add(eng)
        if dma:
            self.dmas_since.add(i)
        else:
            self.since[eng] = i
        for t in list(reads) + list(writes):
            w = self.last_w.get(t)
            if w is not None:
                deps.add(w)
        for t in writes:
            rd = self.readers.get(t)
            if rd:
                deps.update(rd[0].values())
                deps.update(rd[1])
        for t in writes:
            self.last_w[t] = i
            self.readers[t] = ({}, set())
        for t in reads:
            rd = self.readers.setdefault(t, ({}, set()))
            if dma:
                rd[1].add(i)
            else:
                rd[0][eng] = i
        self.ops.append(dict(eng=eng, fn=fn, deps=deps, dma=dma))
        return i

    def emit(self):
        nc = self.nc
        ops = self.ops
        engs = {PE: nc.tensor, ACT: nc.scalar, DVE: nc.vector, POOL: nc.gpsimd, SP: nc.sync}
        signal = [False] * len(ops)
        for i, op in enumerate(ops):
            for j in op["deps"]:
                d = ops[j]
                if d["dma"]:
                    continue
                if d["eng"] == op["eng"] and not op["dma"] and op["eng"] == PE:
                    continue
                signal[j] = True
        tick = {}
        cnt = {e: 0 for e in COMPUTE}
        dman = {}
        dcnt = {}
        for i, op in enumerate(ops):
            if op["dma"] == "cc":
                pass
            elif op["dma"]:
                q = op["eng"]
                dman[i] = dcnt.get(q, 0)
                dcnt[q] = dcnt.get(q, 0) + 1
            elif signal[i]:
                e = op["eng"]
                tick[i] = cnt[e]
                cnt[e] += 1
        sems = {}
        stack = contextlib.ExitStack()

        def getsem(key):
            s = sems.get(key)
            if s is None:
                s = stack.enter_context(nc.semaphore("s_" + "_".join(str(k) for k in key)))
                sems[key] = s
            return s

        def csem(e, t):
            return getsem(("c", e, t // TICKS_PER_SEM)), (t % TICKS_PER_SEM) + 1

        def dsem(q, n):
            gen = n // (NSEM_DMA * DMA_USES_PER_SEM)
            slot = n % NSEM_DMA
            use = (n % (NSEM_DMA * DMA_USES_PER_SEM)) // NSEM_DMA
            return getsem(("d", q, gen, slot)), 16 * (use + 1), use

        ccl = [j for j, o in enumerate(ops) if o["dma"] == "cc"]
        CC_SEMS = max(1, (len(ccl) + 1) // 2)
        ccidx = {j: n for n, j in enumerate(ccl)}

        def asem(j):
            if ops[j]["dma"] == "cc":
                k = ccidx[j]
                return getsem(("cc", k % CC_SEMS)), k // CC_SEMS + 1, 0
            return dsem(ops[j]["eng"], dman[j])

        for i, op in enumerate(ops):
            if op["dma"]:
                asem(i)
            elif signal[i]:
                csem(op["eng"], tick[i])

        def emit_engine(e):
            eng = engs[e]
            waited_c = {}
            waited_d = {}
            for i, op in enumerate(ops):
                if op["eng"] != e:
                    continue
                need_c = {}
                need_d = {}
                for j in op["deps"]:
                    d = ops[j]
                    if d["dma"]:
                        s, v, _ = asem(j)
                        if waited_d.get(id(s), 0) >= v:
                            continue
                        if need_d.get(id(s), (None, 0))[1] < v:
                            need_d[id(s)] = (s, v)
                    else:
                        if d["eng"] == e and e == PE and not op["dma"]:
                            continue
                        t = tick[j]
                        if waited_c.get(d["eng"], -1) >= t:
                            continue
                        if need_c.get(d["eng"], -1) < t:
                            need_c[d["eng"]] = t
                if op["dma"]:
                    s, v, use = asem(i)
                    if use > 0 and waited_d.get(id(s), 0) < v - 16:
                        eng.wait_ge(s, v - 16)
                        waited_d[id(s)] = v - 16
                for de, t in need_c.items():
                    s, v = csem(de, t)
                    eng.wait_ge(s, v)
                    waited_c[de] = t
                for k, (s, v) in need_d.items():
                    eng.wait_ge(s, v)
                    waited_d[k] = v
                ins = op["fn"](eng)
                if op["dma"] == "cc":
                    s, v, _ = asem(i)
                    ins.then_inc(s, 1)
                elif op["dma"]:
                    s, v, _ = asem(i)
                    ins.then_inc(s, 16)
                elif signal[i]:
                    s, v = csem(e, tick[i])
                    ins.then_inc(s, 1)
            last = {}
            for i, op in enumerate(ops):
                if op["dma"] and op["eng"] == e:
                    s, v, _ = asem(i)
                    last[id(s)] = (s, v)
            for s, v in last.values():
                eng.wait_ge(s, v)

        with nc.Block() as block:
            @block.sync
            def _(x):
                emit_engine(SP)

            @block.tensor
            def _(x):
                emit_engine(PE)

            @block.scalar
            def _(x):
                emit_engine(ACT)

            @block.vector
            def _(x):
                emit_engine(DVE)

            @block.gpsimd
            def _(x):
                emit_engine(POOL)
        stack.close()


def build(cfg):
    C, NS, R, BPC, NPG, DFF, NPHYS = (cfg[k] for k in ("C", "NS", "R", "BPC", "NPG", "DFF", "NPHYS"))
    T = 8
    NSMP = BPC * T
    NTP = NS * C
    NTOK = NTP + NSMP
    TPC = C // 128
    NT = R * TPC
    WW = C + 128 * NT
    WLEN = ((WW + 127 + 127) // 128) * 128
    FCH = DFF // 128
    FB = 256
    NFB = DFF // FB
    NB = R * BPC
    XKW = NTP + 2 * NSMP
    groups = [(s * C, C) for s in range(NS)] + [(NTP, NSMP)]
    WMAX = max(C, NSMP)

    nc = bass.Bass("TRN2", target_bir_lowering=False)
    st = contextlib.ExitStack()

    def din(name, shape, dt=F32):
        return nc.dram_tensor(name, list(shape), dt, kind="ExternalInput").ap()

    def dout(name, shape, dt=F32):
        return nc.dram_tensor(name, list(shape), dt, kind="ExternalOutput").ap()

    xT_in = din("xT_in", [D, NTOK])
    wg = [din("wg1", [DEPTH, D, DFF]), din("wg2", [DEPTH, D, DFF])]
    wu = [din("wu1", [DEPTH, D, DFF]), din("wu2", [DEPTH, D, DFF])]
    wd = [din("wd1", [DEPTH, DFF, D]), din("wd2", [DEPTH, DFF, D])]
    w_in = din("w_in", [DEPTH, D, 3072])
    w_out = din("w_out", [DEPTH, D, D])
    lng = din("lng", [DEPTH, 3, 128, KC])
    lnb = din("lnb", [DEPTH, 3, 128, KC])
    convw = din("convw", [DEPTH, 128, 4, 3])
    sconv = din("sconv", [DEPTH, 128, 4, BPC, 2])
    lamp = din("lamp", [DEPTH, 4 * 64])
    subw = din("subw", [DEPTH, 128])
    tab = din("tab", [32, 4])
    tabown = din("tabown", [32, 1])
    sel = din("sel", [33, WLEN])
    sels = din("sels", [33, 512])
    antiI = din("antiI", [128, 128])
    ident = din("ident", [128, 128])
    ptab = din("ptab", [1, NB * NPG], I32)
    rsel = din("rsel", [128, 4])
    hsel = din("hsel", [128, NS * R * NS])
    iotap = din("iotap", [128, 1])
    ckv = [din(f"ckv{l}", [NPHYS * 128, 256]) for l in range(DEPTH)]

    yT = dout("yT", [D, NTOK])
    ko = dout("ko", [DEPTH, NTOK, 512])
    vo = dout("vo", [DEPTH, NTOK, 512])
    convp = dout("convp", [DEPTH, NS, 512, 2])
    convs = dout("convs", [DEPTH, 512, BPC, 2])

    bgs = nc.dram_tensor("bgs", [4, 128, NTOK], F32)
    us = nc.dram_tensor("us", [4, 128, NTOK], F32)
    qs = nc.dram_tensor("qs", [4, 128, NTP], BF16)
    fscr = nc.dram_tensor("fscr", [4, WLEN], F32)
    fscr_s = nc.dram_tensor("fscr_s", [1, 512], F32)
    xk = [[nc.dram_tensor(f"xk{l}_{s}", [512, C if s < NS else 2 * NSMP], BF16) for s in range(NS + 1)]
          for l in range(DEPTH)]
    xk_all = [[nc.dram_tensor(f"xk_all{l}_{s}", [R * 512, C if s < NS else 2 * NSMP], BF16) for s in range(NS + 1)]
              for l in range(DEPTH)]
    xv = [[nc.dram_tensor(f"xv{l}_{s}", [C if s < NS else NSMP, 512], BF16) for s in range(NS + 1)]
          for l in range(DEPTH)]
    xv_all = [[nc.dram_tensor(f"xv_all{l}_{s}", [R * (C if s < NS else NSMP), 512], BF16) for s in range(NS + 1)]
              for l in range(DEPTH)]
    xt = [nc.dram_tensor(f"xt{l}", [NS, 1024], F32) for l in range(DEPTH)]
    xt_all = [nc.dram_tensor(f"xt_all{l}", [R * NS, 1024], F32) for l in range(DEPTH)]
    xz = [nc.dram_tensor(f"xz{l}", [128, NB * T], BF16) for l in range(DEPTH)]
    xz_all = [nc.dram_tensor(f"xz_all{l}", [R * 128, NB * T], BF16) for l in range(DEPTH)]

    def sb(name, shape, dt):
        return st.enter_context(nc.sbuf_tensor(name, list(shape), dt))

    xT = sb("xT", [128, KC, NTOK], F32)
    zTa = sb("zTa", [128, 4, NTOK], BF16)
    onesm = sb("onesm", [128, 128], BF16)
    identS = sb("identS", [128, 128], F32)
    antiS = sb("antiS", [128, 128], F32)
    lngS = sb("lngS", [128, DEPTH * 3 * KC], F32)
    lnbS = sb("lnbS", [128, DEPTH * 3 * KC], F32)
    convwS = sb("convwS", [128, DEPTH * 12], F32)
    b31 = sb("b31", [128, 4], F32)
    b31o = sb("b31o", [128, 1], F32)
    rselS = sb("rselS", [128, 4], F32)
    hselS = sb("hselS", [128, NS * R * NS], F32)
    iotaS = sb("iotaS", [128, 1], F32)
    IDX = sb("IDX", [128, NB * NPG], I32)
    xtl = sb("xtl", [128, R * NS, 8], F32)
    lamS = sb("lamS", [128, 8], F32)
    swS = sb("swS", [128, 128], F32)
    BL = sb("BL", [128, 2 * T], F32)
    Bn = sb("Bn", [128, 2 * T], F32)
    finc = sb("finc", [128, 8], F32)
    lamt = sb("lamt", [128, 256], F32)

    ARENA = cfg.get("ARENA", 112 * 1024)
    arena = sb("arena", [128, ARENA // 4], F32)

    class Carver:
        def __init__(self):
            self.off = 0

        def __call__(self, name, shape, dt):
            P = shape[0]
            n = 1
            for d_ in shape[1:]:
                n *= d_
            esz = 2 if dt == BF16 else 4
            nbytes = ((n * esz + 31) // 32) * 32
            assert self.off + nbytes <= ARENA, (name, self.off, nbytes, ARENA)
            ap = arena[0:P, self.off // 4:(self.off + nbytes) // 4]
            if dt != F32:
                ap = ap.bitcast(dt)
            ap = ap[:, 0:n]
            if len(shape) == 3:
                ap = ap.rearrange("p (a b) -> p a b", a=shape[1])
            elif len(shape) == 4:
                ap = ap.rearrange("p (a b c) -> p a b c", a=shape[1], b=shape[2])
            self.off += nbytes
            return ap

    ca = Carver()
    selS = ca("selS", [33, WLEN], F32)
    tabS = ca("tabS", [33, 4], F32)
    tabo = ca("tabo", [33, 1], F32)
    Fsb = ca("Fsb", [4, WLEN], F32)
    HkS = ca("HkS", [128, 2 * T], F32)
    idxf = ca("idxf", [128, NB * NPG], F32)
    ptB = ca("ptB", [128, NB * NPG], I32)
    ca = Carver()
    xb = ca("xb", [128, KC, WMAX], BF16)
    hT = ca("hT", [128, max(FCH, 2 * KC), WMAX], BF16)
    wbig = [ca(f"wbig{i}", [128, KC, 512], BF16) for i in range(2)]
    wdb = [ca(f"wdb{i}", [128, 2, 512], BF16) for i in range(3)]
    sgt = [ca(f"sgt{i}", [128, WMAX], F32) for i in range(2)]
    stat = [ca(f"stat{i}", [128, WMAX], F32) for i in range(3)]
    st4 = ca("st4", [128, 4, WMAX], F32)
    cgt = ca("cgt", [128, 4, WMAX], F32)
    stb = ca("stb", [128, 4, WMAX], BF16)
    kvst = [ca(f"kvst{i}", [128, 512], F32) for i in range(2)]
    kvb = [ca(f"kvb{i}", [128, 512], BF16) for i in range(2)]
    halo = ca("halo", [128, 4, 2], F32)
    ubuf = ca("ubuf", [128, 4, WMAX + 2 * max(1, BPC)], F32)
    zc = ca("zc", [128, 4, WMAX], BF16)
    dense_top = ca.off
    ca = Carver()
    KTh = ca("KTh", [128, R * NS * C], BF16)
    Vh = ca("Vh", [128, R * NS * TPC, 130], BF16)
    QTh = ca("QTh", [128, NTP], BF16)
    Wwin = ca("Wwin", [128, WW], F32)
    Hk = ca("Hk", [128, 512], F32)
    Pt = [ca(f"Pt{i}", [128, C], BF16) for i in range(4)]
    Stmp = [ca(f"Stmp{i}", [128, C], F32) for i in range(2)]
    fin = [ca(f"fin{i}", [128, 128], F32) for i in range(3)]
    KV = [ca(f"KV{i}", [128, NPG, 256], F32) for i in range(2)]
    KTs = ca("KTs", [128, NPG * 128], BF16)
    Vs = ca("Vs", [128, NPG + 1, 130], BF16)
    Qbd = ca("Qbd", [128, NB, 2 * T], BF16)
    KnT = ca("KnT", [128, NB * T], BF16)
    Ps = ca("Ps", [128, (NPG + 1) * 2 * T], BF16)
    zsT = ca("zsT", [128, NB * T], BF16)
    sq_off = ca.off
    SQt = [ca(f"SQt{i}", [128, R, 2 * NSMP], BF16) for i in range(2)]
    ca_save = ca.off
    ca.off = sq_off
    zall = ca("zall", [128, 4, R, NSMP], BF16)
    ca.off = max(ca_save, ca.off)
    Qsel = ca("Qsel", [128, R, 2 * NSMP], BF16)
    Vn4 = [ca(f"Vn4{i}", [T, 4, 128], BF16) for i in range(2)]
    attn_top = ca.off

    ps = [st.enter_context(nc.psum_tensor(f"ps{i}", [128, 512], F32)) for i in range(8)]

    r = Rec(nc)
    reg_holder = {}

    def spreg(e):
        if "r" not in reg_holder:
            reg_holder["r"] = e.alloc_register("dynreg")
        return reg_holder["r"]

    def dma(q, out, in_, reads, writes, nonc=False):
        if nonc:
            def fn(e):
                with nc.allow_non_contiguous_dma(reason="small strided"):
                    return e.dma_start(out=out, in_=in_)
        else:
            def fn(e):
                return e.dma_start(out=out, in_=in_)
        r.add(q, fn, reads=reads, writes=writes, dma=True)

    def dyn_dma(out, in_fn, idx_ap, reads, writes, nonc=False):
        def fn(e):
            reg_holder["n"] = reg_holder.get("n", 0) + 1
            reg = e.alloc_register(f"dyn{reg_holder['n']}")
            e.reg_load(reg, idx_ap)
            v = e.snap(reg)
            if nonc:
                with nc.allow_non_contiguous_dma(reason="small strided"):
                    ins = e.dma_start(out=out, in_=in_fn(v))
            else:
                ins = e.dma_start(out=out, in_=in_fn(v))
            e.free_register(reg)
            return ins
        r.add(SP, fn, reads=reads, writes=writes, dma=True)

    def mm(out, lhsT, rhs, start, stop, reads, writes):
        r.add(PE, lambda e: e.matmul(out, lhsT=lhsT, rhs=rhs, start=start, stop=stop), reads=reads, writes=writes)

    def act(out, in_, func, reads, writes, scale=None, bias=None, accum=None):
        kw = {}
        if scale is not None:
            kw["scale"] = scale
        if bias is not None:
            kw["bias"] = bias
        if accum is not None:
            kw["accum_out"] = accum
        r.add(ACT, lambda e: e.activation(out=out, in_=in_, func=func, **kw), reads=reads, writes=writes)

    def tt(eng, out, in0, in1, op, reads, writes):
        r.add(eng, lambda e: e.tensor_tensor(out=out, in0=in0, in1=in1, op=op), reads=reads, writes=writes)

    def ts(eng, out, in0, s1, op0, reads, writes, s2=None, op1=None):
        if op1 is None:
            r.add(eng, lambda e: e.tensor_scalar(out=out, in0=in0, scalar1=s1, scalar2=None, op0=op0),
                  reads=reads, writes=writes)
        else:
            r.add(eng, lambda e: e.tensor_scalar(out=out, in0=in0, scalar1=s1, scalar2=s2, op0=op0, op1=op1),
                  reads=reads, writes=writes)

    def stt(out, in0, scalar, in1, op0, op1, reads, writes):
        r.add(DVE, lambda e: e.scalar_tensor_tensor(out=out, in0=in0, scalar=scalar, in1=in1, op0=op0, op1=op1),
              reads=reads, writes=writes)

    def cp(eng, out, in_, reads, writes):
        r.add(eng, lambda e: e.tensor_copy(out=out, in_=in_), reads=reads, writes=writes)

    def memset(eng, ap, val, writes):
        r.add(eng, lambda e: e.memset(ap, val), writes=writes)

    def transpose(out, in_, idn, reads, writes):
        r.add(PE, lambda e: e.transpose(out=out, in_=in_, identity=idn), reads=reads, writes=writes)

    def allgather(src, dst, reads, writes):
        r.add(POOL, lambda e: e.collective_compute(
            "AllGather", ALU.bypass, replica_groups=[list(range(g * R, (g + 1) * R)) for g in range(cfg["NG"])],
            ins=[src.ap().opt()], outs=[dst.ap().opt()]), reads=reads, writes=writes, dma="cc")

    dma(SP, xT[:], xT_in.rearrange("(k p) t -> p k t", p=128), [], ["xT"])
    dma(SP, identS[:], ident, [], ["identS"])
    dma(SP, antiS[:], antiI, [], ["antiS"])
    dma(SP, lngS[:], lng.rearrange("l i p k -> p (l i) k"), [], ["lngS"], nonc=True)
    dma(SP, lnbS[:], lnb.rearrange("l i p k -> p (l i) k"), [], ["lnbS"], nonc=True)
    dma(SP, convwS[:], convw.rearrange("l p c j -> p l (c j)"), [], ["convwS"], nonc=True)
    dma(SP, rselS[:], rsel, [], ["rselS"])
    dma(SP, hselS[:], hsel, [], ["hselS"])
    dma(SP, iotaS[:], iotap, [], ["iotaS"])
    dma(SP, ptB[:], ptab.partition_broadcast(128), [], ["ptB"])
    dma(SP, selS[:], sel, [], ["selS"])
    dma(SP, tabS[0:32, :], tab, [], ["tabS"])
    dma(SP, tabo[0:32, :], tabown, [], ["tabo"])
    dma(SP, b31[:], tab[31:32, :].partition_broadcast(128), [], ["b31"])
    dma(SP, b31o[:], tabown[31:32, :].partition_broadcast(128), [], ["b31o"])
    memset(DVE, onesm[:], 1.0 / 1024.0, ["onesm"])
    memset(DVE, tabS[32:33, :], NEG, ["tabS"])
    memset(DVE, tabo[32:33, :], NEG, ["tabo"])

    for x0 in range(0, WLEN, 512):
        xw = min(512, WLEN - x0)
        mm(ps[0][0:4, 0:xw], tabS[:, :], selS[:, x0:x0 + xw], True, True, ["tabS", "selS"], ["ps0"])
        cp(DVE, Fsb[:, x0:x0 + xw], ps[0][0:4, 0:xw], ["ps0"], ["Fsb"])
    dma(SP, fscr[:, :], Fsb[:], ["Fsb"], ["fscr"])
    dma(SP, selS[:, 0:512], sels, ["selS", "Fsb"], ["selS2"])
    mm(ps[1][0:1, 0:512], tabo[:, :], selS[:, 0:512], True, True, ["tabo", "selS2"], ["ps1"])
    cp(DVE, Fsb[0:1, 0:512], ps[1][0:1, 0:512], ["ps1", "fscr"], ["Fsb2"])
    dma(SP, fscr_s[:, :], Fsb[0:1, 0:512], ["Fsb2"], ["fscr_s"])
    for which, off in ((BL, 0), (Bn, 256)):
        dma(SP, HkS[:, 0:T], bass.AP(fscr_s.ap().tensor, off, [[1, 128], [1, T]]),
            ["fscr_s", "psS"], ["HkS"])
        mm(ps[2][:, 0:T], antiS[:], HkS[:, 0:T], True, True, ["antiS", "HkS"], ["ps2"])
        cp(DVE, which[:, 0:T], ps[2][:, 0:T], ["ps2"], ["psS", "Bsm"])
        cp(DVE, which[:, T:2 * T], ps[2][:, 0:T], ["ps2"], ["psS", "Bsm"])

    cp(DVE, idxf[:, :], ptB[:, :], ["ptB"], ["idxf"])
    ts(DVE, idxf[:, :], idxf[:, :], 128.0, ALU.mult, ["idxf", "iotaS"], ["idxf"], s2=iotaS[:, 0:1], op1=ALU.add)
    cp(DVE, IDX[:, :], idxf[:, :], ["idxf"], ["IDX"])
    r.barrier()
    def make_xb(c0, w):
        act(xb[:, :, 0:w], xT[:, :, c0:c0 + w], AF.Copy, ["xT"], ["xb"])

    def layer_norm(l, i, c0, w):
        rb = hT[:, 0:KC, 0:w]
        rsq = hT[:, KC:2 * KC, 0:w]
        act(rb, xT[:, :, c0:c0 + w], AF.Copy, ["xT"], ["hT"])
        act(rsq, xT[:, :, c0:c0 + w], AF.Square, ["xT"], ["hT"])
        rsq_f = lambda k: hT[:, KC + k, 0:w]
        rtok = "hT"
        for k in range(KC):
            mm(ps[0][:, 0:w], onesm[:], hT[:, k, 0:w], k == 0, k == KC - 1, ["onesm", "hT"], ["ps0"])
        for k in range(KC):
            mm(ps[1][:, 0:w], onesm[:], rsq_f(k), k == 0, k == KC - 1, ["onesm", rtok], ["ps1"])
        mean, var, rstd = stat[0][:, 0:w], stat[1][:, 0:w], stat[2][:, 0:w]
        cp(DVE, mean, ps[0][:, 0:w], ["ps0"], ["stat0"])
        tt(DVE, var, mean, mean, ALU.mult, ["stat0"], ["stat1"])
        tt(DVE, var, ps[1][:, 0:w], var, ALU.subtract, ["ps1", "stat1"], ["stat1"])
        ts(DVE, var, var, LN_EPS, ALU.add, ["stat1"], ["stat1"])
        act(var, var, AF.Sqrt, ["stat1"], ["stat1"])
        r.add(DVE, lambda e, rstd=rstd, var=var: e.reciprocal(out=rstd, in_=var), reads=["stat1"], writes=["stat2"])
        gi = (l * 3 + i) * KC
        for k in range(KC):
            xs = xT[:, k, c0:c0 + w]
            tt(DVE, xs, xs, mean, ALU.subtract, ["xT", "stat0"], ["xT"])
            tt(DVE, xs, xs, rstd, ALU.mult, ["xT", "stat2"], ["xT"])
            ts(DVE, xs, xs, lngS[:, gi + k:gi + k + 1], ALU.mult, ["xT", "lngS", "lnbS"], ["xT"],
               s2=lnbS[:, gi + k:gi + k + 1], op1=ALU.add)
        make_xb(c0, w)

    wcnt = {"g": 0, "d": 0, "i": 0, "s": 0, "kv": 0}
    wgb = [wbig[i][:, :, 0:256] for i in range(2)]
    wub = [wbig[i][:, :, 256:512] for i in range(2)]
    winb = wbig

    def ffn(l, which, c0, w):
        wgl = wg[which][l].rearrange("(k p) f -> p k f", p=128)
        wul = wu[which][l].rearrange("(k p) f -> p k f", p=128)
        wdl = wd[which][l].rearrange("(f p) o -> p f o", p=128)
        for fb in range(NFB):
            bi = wcnt["i"] % 2
            wcnt["i"] += 1
            dma(POOL, wgb[bi][:], wgl[:, :, fb * FB:(fb + 1) * FB], [], [f"winb{bi}"])
            dma(POOL, wub[bi][:], wul[:, :, fb * FB:(fb + 1) * FB], [], [f"winb{bi}"])
            for fc in range(FB // 128):
                f = fb * (FB // 128) + fc
                pg, pu = ps[f % 2], ps[2 + f % 2]
                for k in range(KC):
                    mm(pg[:, 0:w], wgb[bi][:, k, fc * 128:(fc + 1) * 128], xb[:, k, 0:w], k == 0, k == KC - 1,
                       [f"winb{bi}", "xb"], [f"ps{f % 2}"])
                for k in range(KC):
                    mm(pu[:, 0:w], wub[bi][:, k, fc * 128:(fc + 1) * 128], xb[:, k, 0:w], k == 0, k == KC - 1,
                       [f"winb{bi}", "xb"], [f"ps{2 + f % 2}"])
                si = wcnt["s"] % 2
                wcnt["s"] += 1
                act(sgt[si][:, 0:w], pg[:, 0:w], AF.Silu, [f"ps{f % 2}"], [f"sgt{si}"])
                tt(DVE, hT[:, f, 0:w], sgt[si][:, 0:w], pu[:, 0:w], ALU.mult, [f"sgt{si}", f"ps{2 + f % 2}"], ["hT"])
        ckpt(0.2)
        for half in range(2):
            for f2 in range(0, FCH, 2):
                bi = wcnt["d"] % 3
                wcnt["d"] += 1
                nf = min(2, FCH - f2)
                dma(POOL, wdb[bi][:, 0:nf, :], wdl[:, f2:f2 + nf, half * 512:(half + 1) * 512], [], [f"wdb{bi}"])
                for ff in range(nf):
                    f = f2 + ff
                    for o in range(4):
                        mm(ps[4 + o][:, 0:w], wdb[bi][:, ff, o * 128:(o + 1) * 128], hT[:, f, 0:w],
                           f == 0, f == FCH - 1, [f"wdb{bi}", "hT"], [f"ps{4 + o}"])
            for o in range(4):
                oc = half * 4 + o
                xs = xT[:, oc, c0:c0 + w]
                ts(DVE, xs, xs, ALPHA, ALU.mult, ["xT"], ["xT"])
                stt(xs, ps[4 + o][:, 0:w], 0.5, xs, ALU.mult, ALU.add, [f"ps{4 + o}", "xT"], ["xT"])

    STOP = cfg.get('STOP', 99)

    def ckpt(level):
        if STOP <= level:
            raise StopBuild()

    try:
      for l in range(DEPTH):
          if l == 0:
              ckpt(0)
          lam_init = 0.8 - 0.6 * math.exp(-0.3 * l)
          dma(SP, lamt[:, :], lamp[l:l + 1, :].partition_broadcast(128), ["lamS"], ["lamt"])
          tt(DVE, lamt[:, 0:64], lamt[:, 0:64], lamt[:, 64:128], ALU.mult, ["lamt"], ["lamt"])
          tt(DVE, lamt[:, 128:192], lamt[:, 128:192], lamt[:, 192:256], ALU.mult, ["lamt"], ["lamt"])
          r.add(DVE, lambda e: e.reduce_sum(out=lamS[:, 0:1], in_=lamt[:, 0:64], axis=mybir.AxisListType.X),
                reads=["lamt"], writes=["lamS"])
          r.add(DVE, lambda e: e.reduce_sum(out=lamS[:, 1:2], in_=lamt[:, 128:192], axis=mybir.AxisListType.X),
                reads=["lamt"], writes=["lamS"])
          act(lamS[:, 2:4], lamS[:, 0:2], AF.Exp, ["lamS"], ["lamS"])
          tt(DVE, lamS[:, 4:5], lamS[:, 2:3], lamS[:, 3:4], ALU.subtract, ["lamS"], ["lamS"])
          ts(DVE, lamS[:, 4:5], lamS[:, 4:5], lam_init, ALU.add, ["lamS"], ["lamS"])
          dma(SP, swS[:, :], subw[l:l + 1, :].partition_broadcast(128), ["swS"], ["swS"])
          ts(DVE, swS[:, :], swS[:, :], 1.0 - lam_init, ALU.mult, ["swS"], ["swS"])
          lamcol = lamS[:, 4:5]

          winl = w_in[l].rearrange("(k p) f -> p k f", p=128)
          for gi, (c0, w) in enumerate(groups):
              make_xb(c0, w)
              ckpt(0.1 + 10 * l)
              ffn(l, 0, c0, w)
              ckpt(0.4 + 10 * l)
              layer_norm(l, 0, c0, w)
              ckpt(0.6 + 10 * l)
              is_s = gi == NS
              for blk in range(6):
                  bi = wcnt["i"] % 2
                  wcnt["i"] += 1
                  dma(POOL, winb[bi][:], winl[:, :, blk * 512:(blk + 1) * 512], [], [f"winb{bi}"])
                  wb = winb[bi]
                  if blk <= 4:
                      for ch in range(4):
                          pp = ps[ch % 4]
                          for k in range(KC):
                              mm(pp[:, 0:w], wb[:, k, ch * 128:(ch + 1) * 128], xb[:, k, 0:w], k == 0, k == KC - 1,
                                 [f"winb{bi}", "xb"], [f"ps{ch % 4}"])
                          pt_ = f"ps{ch % 4}"
                          if blk == 0:
                              act(st4[:, ch, 0:w], pp[:, 0:w], AF.Copy, [pt_], ["st4"])
                          elif blk == 1:
                              act(cgt[:, ch, 0:w], pp[:, 0:w], AF.Copy, [pt_], ["cgt"])
                          elif blk == 2:
                              tt(DVE, st4[:, ch, 0:w], pp[:, 0:w], cgt[:, ch, 0:w], ALU.mult, [pt_, "cgt"], ["st4"])
                          elif blk == 3:
                              act(stb[:, ch, 0:w], pp[:, 0:w], AF.Identity, [pt_], ["stb"], scale=0.125)
                          else:
                              act(stb[:, ch, 0:w], pp[:, 0:w], AF.Copy, [pt_], ["stb"])
                      if blk == 0:
                          dma(SP, bgs[:, :, c0:c0 + w].rearrange("c p t -> p c t"), st4[:, :, 0:w], ["st4"], ["bgs"])
                      elif blk == 2:
                          dma(SP, us[:, :, c0:c0 + w].rearrange("c p t -> p c t"), st4[:, :, 0:w], ["st4"], ["us"])
                          if not is_s:
                              dma(SP, xt[l][gi:gi + 1, :].rearrange("o (c p two) -> p (o c) two", p=128, two=2),
                                  st4[:, :, w - 2:w], ["st4"], [f"xt{l}"], nonc=True)
                              dma(SP, convp[l, gi].rearrange("(c p) two -> p c two", p=128),
                                  st4[:, :, w - 2:w], ["st4"], [], nonc=True)
                          else:
                              for cc in range(4):
                                  dma(SP, convs[l, cc * 128:(cc + 1) * 128, :, :],
                                      st4[:, cc, 0:w].rearrange("p (b t) -> p b t", t=T)[:, :, T - 2:T],
                                      ["st4"], [], nonc=True)
                      elif blk == 3:
                          if not is_s:
                              dma(SP, qs[:, :, c0:c0 + w].rearrange("h p t -> p h t"), stb[:, :, 0:w], ["stb"], ["qs"])
                          else:
                              dma(SP, xk[l][NS][:, NSMP:2 * NSMP].rearrange("(h p) t -> p h t", p=128),
                                  stb[:, :, 0:w], ["stb"], [f"xk{l}_{NS}"])
                      elif blk == 4:
                          dma(SP, xk[l][gi][:, 0:w].rearrange("(h p) t -> p h t", p=128),
                              stb[:, :, 0:w], ["stb"], [f"xk{l}_{gi}"])
                  ckpt(0.70 + 0.01 * blk)
                  if blk >= 4:
                      dst = ko if blk == 4 else vo
                      for t0 in range(0, w, 128):
                          tw = min(128, w - t0)
                          pp = ps[4 + (wcnt["kv"] % 4)]
                          ptk = f"ps{4 + (wcnt['kv'] % 4)}"
                          si = wcnt["kv"] % 2
                          wcnt["kv"] += 1
                          DBG = cfg.get("DBG", "")
                          if "nomm" not in DBG:
                              if "n256" in DBG:
                                  for hf in range(2):
                                      for k in range(KC):
                                          mm(pp[0:tw, hf * 256:(hf + 1) * 256], xb[:, k, t0:t0 + tw],
                                             wb[:, k, hf * 256:(hf + 1) * 256], k == 0 and hf == 0,
                                             k == KC - 1 and hf == 1, [f"winb{bi}", "xb"], [ptk])
                              else:
                                  for k in range(KC):
                                      mm(pp[0:tw, :], xb[:, k, t0:t0 + tw], wb[:, k, :], k == 0, k == KC - 1,
                                         [f"winb{bi}", "xb"], [ptk])
                          if "nocp" not in DBG:
                              act(kvst[si][0:tw, :], pp[0:tw, :], AF.Copy, [ptk], [f"kvst{si}"])
                          if "nodma" not in DBG:
                              dma(SP, dst[l, c0 + t0:c0 + t0 + tw, :], kvst[si][0:tw, :], [f"kvst{si}"], [])
                          if blk == 5:
                              act(kvb[si][0:tw, :], pp[0:tw, :], AF.Copy, [ptk], [f"kvb{si}"])
                              dma(SP, xv[l][gi][t0:t0 + tw, :], kvb[si][0:tw, :], [f"kvb{si}"], [f"xv{l}_{gi}"])
                  ckpt(0.75 + 0.002 * (blk - 4) + 0.01 * gi if blk >= 4 else -1)

          ckpt(1 + 10 * l)
          for s_ in range(NS + 1):
              allgather(xk[l][s_], xk_all[l][s_], [f"xk{l}_{s_}"], [f"xk_all{l}_{s_}"])
              allgather(xv[l][s_], xv_all[l][s_], [f"xv{l}_{s_}"], [f"xv_all{l}_{s_}"])
          allgather(xt[l], xt_all[l], [f"xt{l}"], [f"xt_all{l}"])

          ckpt(2 + 10 * l)
          r.barrier()
          memset(DVE, Vh[:, :, 128:130], 1.0, ["Vh"])
          memset(DVE, Vs[:, :, 128:130], 1.0, ["Vs"])
          memset(DVE, Qbd[:, :, :], 0.0, ["Qbd"])
          for h in range(4):
              if True:
                  for x0 in range(0, WW, 512):
                      xw = min(512, WW - x0)
                      dma(SP, Hk[:, 0:xw], bass.AP(fscr.ap().tensor, h * WLEN + x0, [[1, 128], [1, xw]]), ["fscr"], ["Hk"])
                      mm(ps[0][:, 0:xw], antiS[:], Hk[:, 0:xw], True, True, ["antiS", "Hk"], ["ps0"])
                      cp(DVE, Wwin[:, x0:x0 + xw], ps[0][:, 0:xw], ["ps0"], ["Wwin"])
              for rr in range(R):
                  for s in range(NS):
                      k0 = (s * R + rr) * C
                      dma(SP, KTh[:, k0:k0 + C],
                          xk_all[l][s][rr * 512 + h * 128:rr * 512 + (h + 1) * 128, 0:C],
                          [f"xk_all{l}_{s}"], ["KTh"])
                      kt0 = (s * R + rr) * TPC
                      dma(SP, Vh[:, kt0:kt0 + TPC, 0:128],
                          xv_all[l][s][rr * C:(rr + 1) * C, h * 128:(h + 1) * 128]
                          .rearrange("(w p) d -> p w d", p=128), [f"xv_all{l}_{s}"], ["Vh"])
              dma(SP, QTh[:, :], qs[h, :, :], ["qs"], ["QTh"])
              ucount = 0
              for s in range(NS):
                  nkt = NT * (s + 1)
                  for kt in range(nkt):
                      near = kt >= NT * s - 1
                      for c in range(2):
                          sp_i = (ucount % 2) * 2 + c
                          mm(ps[sp_i][:, 0:C], KTh[64 * c:64 * c + 64, kt * 128:(kt + 1) * 128],
                             QTh[64 * c:64 * c + 64, s * C:(s + 1) * C], True, True, ["KTh", "QTh"], [f"ps{sp_i}"])
                          if near:
                              t = kt - NT * s
                              off = 128 * (NT - 1 - t)
                              tt(DVE, Stmp[c][:, :], ps[sp_i][:, 0:C], Wwin[:, off:off + C], ALU.add,
                                 [f"ps{sp_i}", "Wwin"], [f"Stmp{c}"])
                              act(Pt[sp_i][:, :], Stmp[c][:, :], AF.Exp, [f"Stmp{c}"], [f"Pt{sp_i}"])
                          else:
                              act(Pt[sp_i][:, :], ps[sp_i][:, 0:C], AF.Exp, [f"ps{sp_i}", "b31"], [f"Pt{sp_i}"],
                                  bias=b31[:, h:h + 1])
                      for c in range(2):
                          sp_i = (ucount % 2) * 2 + c
                          for qb in range(TPC):
                              a = qb * 2 + c
                              ob = 4 + a // 2
                              oo = (a % 2) * 256
                              mm(ps[ob][:, oo:oo + 129], Pt[sp_i][:, qb * 128:(qb + 1) * 128], Vh[:, kt, 0:129],
                                 kt == 0 and c == 0, kt == nkt - 1 and c == 1, [f"Pt{sp_i}", "Vh"], [f"ps{ob}"])
                      ucount += 1
                  for qb in range(TPC):
                      o0 = ps[4 + (qb * 2) // 2][:, 0:129]
                      o1 = ps[4 + (qb * 2 + 1) // 2][:, 256:256 + 129]
                      pt0, pt1 = f"ps{4 + qb}", f"ps{4 + qb}"
                      r.add(DVE, lambda e, o0=o0: e.reciprocal(out=finc[:, 0:1], in_=o0[:, 128:129]),
                            reads=[pt0], writes=["finc"])
                      r.add(DVE, lambda e, o1=o1: e.reciprocal(out=finc[:, 1:2], in_=o1[:, 128:129]),
                            reads=[pt1], writes=["finc"])
                      tt(DVE, finc[:, 1:2], finc[:, 1:2], lamcol, ALU.mult, ["finc", "lamS"], ["finc"])
                      ts(DVE, fin[0][:, :], o1[:, 0:128], finc[:, 1:2], ALU.mult, [pt1, "finc"], ["fin0"])
                      stt(fin[1][:, :], o0[:, 0:128], finc[:, 0:1], fin[0][:, :], ALU.mult, ALU.subtract,
                          [pt0, "finc", "fin0"], ["fin1"])
                      act(fin[0][:, :], fin[1][:, :], AF.Square, ["fin1"], ["fin0", "finc2"], accum=finc[:, 2:3])
                      ts(DVE, finc[:, 3:4], finc[:, 2:3], 1.0 / 128.0, ALU.mult, ["finc2"], ["finc3"], s2=LN_EPS, op1=ALU.add)
                      act(finc[:, 3:4], finc[:, 3:4], AF.Sqrt, ["finc3"], ["finc3"])
                      r.add(DVE, lambda e: e.reciprocal(out=finc[:, 3:4], in_=finc[:, 3:4]), reads=["finc3"], writes=["finc3"])
                      stt(fin[2][:, :], fin[1][:, :], finc[:, 3:4], swS[:, :], ALU.mult, ALU.mult,
                          ["fin1", "finc3", "swS"], ["fin2"])
                      transpose(ps[0][:, 0:128], fin[2][:, :], identS[:], ["fin2", "identS"], ["ps0"])
                      col = s * C + qb * 128
                      act(zTa[:, h, col:col + 128], ps[0][:, 0:128], AF.Copy, ["ps0"], ["zTa"])

          ckpt(3 + 10 * l)
          for hh in range(4):
              dma(SP, SQt[hh % 2][:, :, :],
                  xk_all[l][NS].ap().rearrange("(r h p) t -> h p r t", r=R, h=4)[hh, :, :, :],
                  [f"xk_all{l}_{NS}"], [f"SQt{hh % 2}"])
              if hh == 0:
                  ts(DVE, Qsel[:, :, :], SQt[0][:, :, :], rselS[:, 0:1], ALU.mult, ["SQt0", "rselS"], ["Qsel"])
              else:
                  r.add(DVE, lambda e, hh=hh: e.scalar_tensor_tensor(
                      out=Qsel[:, :, :], in0=SQt[hh % 2][:, :, :], scalar=rselS[:, hh:hh + 1], in1=Qsel[:, :, :],
                      op0=ALU.mult, op1=ALU.add), reads=[f"SQt{hh % 2}", "rselS", "Qsel"], writes=["Qsel"])
          cp(DVE, KnT[:, :].rearrange("p (r t) -> p r t", r=R), Qsel[:, :, 0:NSMP], ["Qsel"], ["KnT"])
          for c in range(2):
              for rr in range(R):
                  cp(DVE, Qbd[64 * c:64 * c + 64, rr * BPC:(rr + 1) * BPC, c * T:(c + 1) * T],
                     Qsel[64 * c:64 * c + 64, rr, NSMP:2 * NSMP].rearrange("p (b t) -> p b t", t=T), ["Qsel"], ["Qbd"])
          for b in range(NB):
              rb_, bb = b // BPC, b % BPC
              bi = b % 2
              for j in range(NPG):
                  col = b * NPG + j
                  r.add(POOL, lambda e, bi=bi, j=j, col=col, l=l: e.indirect_dma_start(
                      out=KV[bi][:, j, :], out_offset=None, in_=ckv[l],
                      in_offset=bass.IndirectOffsetOnAxis(ap=IDX[:, col:col + 1], axis=0)),
                      reads=["IDX"], writes=[f"KV{bi}"], dma=True)
              vi = b % 2
              row0 = rb_ * NSMP + bb * T
              dma(SP, Vn4[vi][:, :, :], xv_all[l][NS][row0:row0 + T, :].rearrange("n (h d) -> n h d", h=4),
                  [f"xv_all{l}_{NS}"], [f"Vn4{vi}"])
              ts(DVE, Vs[0:T, NPG, 0:128], Vn4[vi][:, 0, :], rselS[0:T, 0:1], ALU.mult, [f"Vn4{vi}", "rselS"], ["Vs"])
              for hh in range(1, 4):
                  r.add(DVE, lambda e, hh=hh, vi=vi: e.scalar_tensor_tensor(
                      out=Vs[0:T, NPG, 0:128], in0=Vn4[vi][:, hh, :], scalar=rselS[0:T, hh:hh + 1], in1=Vs[0:T, NPG, 0:128],
                      op0=ALU.mult, op1=ALU.add), reads=[f"Vn4{vi}", "rselS", "Vs"], writes=["Vs"])
              ckpt(3.1 + 10 * l)
              cp(POOL, Vs[:, 0:NPG, 0:128], KV[bi][:, :, 128:256], [f"KV{bi}"], ["Vs"])
              ckpt(3.2 + 10 * l)
              for j0 in range(0, NPG, 4):
                  nj = min(4, NPG - j0)
                  pb = ps[(j0 // 4) % 2]
                  for jj in range(nj):
                      transpose(pb[:, jj * 128:(jj + 1) * 128], KV[bi][:, j0 + jj, 0:128], identS[:],
                                [f"KV{bi}", "identS"], [f"ps{(j0 // 4) % 2}"])
                  act(KTs[:, j0 * 128:(j0 + nj) * 128], pb[:, 0:nj * 128], AF.Copy, [f"ps{(j0 // 4) % 2}"], ["KTs"])
              ckpt(3.3 + 10 * l)
              SS = ps[2]
              for j in range(NPG):
                  mm(SS[:, j * 16:(j + 1) * 16], KTs[:, j * 128:(j + 1) * 128], Qbd[:, b, :], True, True,
                     ["KTs", "Qbd"], ["ps2"])
              mm(SS[0:T, NPG * 16:(NPG + 1) * 16], KnT[:, b * T:(b + 1) * T], Qbd[:, b, :], True, True,
                 ["KnT", "Qbd"], ["ps2"])
              ckpt(3.4 + 10 * l)
              DBG = cfg.get("DBG", "")
              ser = ["ssser"] if "ser" in DBG else []
              if NPG > 1:
                  act(Ps[:, 0:(NPG - 1) * 16], SS[:, 0:(NPG - 1) * 16], AF.Exp, ["ps2", "b31o"], ["Ps"] + ser, bias=b31o[:, 0:1])
              if "actev" in DBG:
                  act(Stmp[1][:, 0:32], SS[:, (NPG - 1) * 16:(NPG + 1) * 16], AF.Copy, ["ps2"], ["Stmp1"] + ser)
                  tt(DVE, Stmp[0][:, 0:16], Stmp[1][:, 0:16], BL[:, :], ALU.add, ["Stmp1", "Bsm"], ["Stmp0"])
                  tt(DVE, Stmp[0][0:T, 16:32], Stmp[1][0:T, 16:32], Bn[0:T, :], ALU.add, ["Stmp1", "Bsm"], ["Stmp0"])
              else:
                  tt(DVE, Stmp[0][:, 0:16], SS[:, (NPG - 1) * 16:NPG * 16], BL[:, :], ALU.add, ["ps2", "Bsm"], ["Stmp0"] + ser)
                  tt(DVE, Stmp[0][0:T, 16:32], SS[0:T, NPG * 16:(NPG + 1) * 16], Bn[0:T, :], ALU.add, ["ps2", "Bsm"], ["Stmp0"] + ser)
              act(Ps[:, (NPG - 1) * 16:NPG * 16], Stmp[0][:, 0:16], AF.Exp, ["Stmp0"], ["Ps"])
              act(Ps[0:T, NPG * 16:(NPG + 1) * 16], Stmp[0][0:T, 16:32], AF.Exp, ["Stmp0"], ["Ps"])
              ckpt(3.5 + 10 * l)
              OS = ps[3]
              for c in range(2):
                  for j in range(NPG):
                      mm(OS[0:T, c * 256:c * 256 + 129], Ps[:, j * 16 + c * T:j * 16 + (c + 1) * T], Vs[:, j, 0:129],
                         j == 0, False, ["Ps", "Vs"], ["ps3"])
                  mm(OS[0:T, c * 256:c * 256 + 129], Ps[0:T, NPG * 16 + c * T:NPG * 16 + (c + 1) * T], Vs[0:T, NPG, 0:129],
                     False, True, ["Ps", "Vs"], ["ps3"])
              ckpt(3.6 + 10 * l)
              o0 = OS[0:T, 0:129]
              o1 = OS[0:T, 256:256 + 129]
              r.add(DVE, lambda e, o0=o0: e.reciprocal(out=finc[0:T, 0:1], in_=o0[:, 128:129]), reads=["ps3"], writes=["finc"])
              r.add(DVE, lambda e, o1=o1: e.reciprocal(out=finc[0:T, 1:2], in_=o1[:, 128:129]), reads=["ps3"], writes=["finc"])
              tt(DVE, finc[0:T, 1:2], finc[0:T, 1:2], lamcol[0:T, :], ALU.mult, ["finc", "lamS"], ["finc"])
              ts(DVE, fin[0][0:T, :], o1[:, 0:128], finc[0:T, 1:2], ALU.mult, ["ps3", "finc"], ["fin0"])
              stt(fin[1][0:T, :], o0[:, 0:128], finc[0:T, 0:1], fin[0][0:T, :], ALU.mult, ALU.subtract,
                  ["ps3", "finc", "fin0"], ["fin1"])
              act(fin[0][0:T, :], fin[1][0:T, :], AF.Square, ["fin1"], ["fin0", "finc2"], accum=finc[0:T, 2:3])
              ts(DVE, finc[0:T, 3:4], finc[0:T, 2:3], 1.0 / 128.0, ALU.mult, ["finc2"], ["finc3"], s2=LN_EPS, op1=ALU.add)
              act(finc[0:T, 3:4], finc[0:T, 3:4], AF.Sqrt, ["finc3"], ["finc3"])
              r.add(DVE, lambda e: e.reciprocal(out=finc[0:T, 3:4], in_=finc[0:T, 3:4]), reads=["finc3"], writes=["finc3"])
              stt(fin[2][0:T, :], fin[1][0:T, :], finc[0:T, 3:4], swS[0:T, :], ALU.mult, ALU.mult,
                  ["fin1", "finc3", "swS"], ["fin2"])
              transpose(ps[4][:, b * T:(b + 1) * T], fin[2][0:T, :], identS[0:T, 0:T], ["fin2", "identS"], ["ps4"])
          act(zsT[:, :], ps[4][:, 0:NB * T], AF.Copy, ["ps4"], ["zsT"])
          dma(SP, xz[l][:, :], zsT[:, :], ["zsT"], [f"xz{l}"])
          allgather(xz[l], xz_all[l], [f"xz{l}"], [f"xz_all{l}"])
          dma(SP, zall[:, :, :, :].rearrange("p h r t -> p h (r t)"),
              xz_all[l].ap().rearrange("(h p) t -> p h t", p=128), [f"xz_all{l}"], ["zall", "SQt0", "SQt1"])
          ts(DVE, zTa[:, :, NTP:NTOK], zall[:, :, 0, :], rselS[:, 0:1], ALU.mult, ["zall", "rselS"], ["zTa"])
          for rr in range(1, R):
              r.add(DVE, lambda e, rr=rr: e.scalar_tensor_tensor(
                  out=zTa[:, :, NTP:NTOK], in0=zall[:, :, rr, :], scalar=rselS[:, rr:rr + 1], in1=zTa[:, :, NTP:NTOK],
                  op0=ALU.mult, op1=ALU.add), reads=["zall", "rselS", "zTa"], writes=["zTa"])
          for rw in range(R * NS):
              dma(SP, xtl[:, rw, :].rearrange("p (c two) -> p c two", two=2),
                  xt_all[l][rw:rw + 1, :].rearrange("o (c p two) -> p (o c) two", p=128, two=2),
                  [f"xt_all{l}"], ["xtl"], nonc=True)

          ckpt(4 + 10 * l)
          r.barrier()
          woutl = w_out[l].rearrange("(k p) f -> p k f", p=128)
          for gi, (c0, w) in enumerate(groups):
              is_s = gi == NS
              nb_ = BPC if is_s else 1
              tl = T if is_s else w
              U = ubuf[:, :, 0:nb_ * (tl + 2)].rearrange("p c (b t) -> p c b t", b=nb_)
              for cc in range(4):
                  dma(SP, U[:, cc, :, 2:tl + 2], us[cc, :, c0:c0 + w].rearrange("p (b t) -> p b t", b=nb_), ["us"], ["ubuf"],
                      nonc=is_s)
              dma(SP, st4[:, :, 0:w], bgs[:, :, c0:c0 + w].rearrange("c p t -> p c t"), ["bgs"], ["st4"])
              if is_s:
                  for cc in range(4):
                      dma(SP, U[:, cc, :, 0:2], sconv[l, :, cc, :, :], [], ["ubuf"], nonc=True)
              else:
                  hv = halo[:, :, :].rearrange("p c two -> p (c two)")
                  ts(DVE, hv, xtl[:, 0, :], hselS[:, gi * R * NS:gi * R * NS + 1], ALU.mult, ["xtl", "hselS"], ["halo"])
                  for rw in range(1, R * NS):
                      r.add(DVE, lambda e, rw=rw, hv=hv, gi=gi: e.scalar_tensor_tensor(
                          out=hv, in0=xtl[:, rw, :], scalar=hselS[:, gi * R * NS + rw:gi * R * NS + rw + 1], in1=hv,
                          op0=ALU.mult, op1=ALU.add), reads=["xtl", "hselS", "halo"], writes=["halo"])
                  cp(DVE, U[:, :, 0, 0:2], halo[:, :, :], ["halo"], ["ubuf"])
              for cc in range(4):
                  wi = l * 12 + cc * 3
                  acc = cgt[:, cc, 0:w].rearrange("p (b t) -> p b t", b=nb_)
                  ts(DVE, acc, U[:, cc, :, 0:tl], convwS[:, wi:wi + 1], ALU.mult, ["ubuf", "convwS"], ["cgt"])
                  for jx in (1, 2):
                      r.add(DVE, lambda e, acc=acc, cc=cc, jx=jx, wi=wi, U=U, tl=tl: e.scalar_tensor_tensor(
                          out=acc, in0=U[:, cc, :, jx:jx + tl], scalar=convwS[:, wi + jx:wi + jx + 1], in1=acc,
                          op0=ALU.mult, op1=ALU.add), reads=["ubuf", "convwS", "cgt"], writes=["cgt"])
                  tt(DVE, zc[:, cc, 0:w], cgt[:, cc, 0:w], st4[:, cc, 0:w], ALU.mult, ["cgt", "st4"], ["zc"])
              for half in range(2):
                  bi = wcnt["i"] % 2
                  wcnt["i"] += 1
                  dma(POOL, winb[bi][:], woutl[:, :, half * 512:(half + 1) * 512], [], [f"winb{bi}"])
                  for o in range(4):
                      oc = half * 4 + o
                      for k in range(KC):
                          zr = zc[:, k, 0:w] if k < 4 else zTa[:, k - 4, c0:c0 + w]
                          mm(ps[4 + o][:, 0:w], winb[bi][:, k, o * 128:(o + 1) * 128], zr,
                             k == 0, k == KC - 1, [f"winb{bi}", "zc", "zTa"], [f"ps{4 + o}"])
                      xs = xT[:, oc, c0:c0 + w]
                      stt(xs, xs, ALPHA, ps[4 + o][:, 0:w], ALU.mult, ALU.add, [f"ps{4 + o}", "xT"], ["xT"])
              layer_norm(l, 1, c0, w)
              ffn(l, 1, c0, w)
              layer_norm(l, 2, c0, w)
              if l == DEPTH - 1:
                  dma(SP, yT.rearrange("(k p) t -> p k t", p=128)[:, :, c0:c0 + w], xT[:, :, c0:c0 + w], ["xT"], [])

    except StopBuild:
        pass
    r.emit()
    return nc


def _bucket(n):
    n = np.asarray(n)
    nn = np.maximum(n, 0)
    nf = np.maximum(nn, 1).astype(np.float32)
    large = 16 + (np.log(nf / np.float32(16)) / np.float32(math.log(128 / 16)) * np.float32(16)).astype(np.int32)
    large = np.minimum(large, 31)
    return np.where(nn < 16, nn, large)


def _sel_rows(dist):
    dist = np.asarray(dist)
    row = np.where(dist < 0, 32, _bucket(dist))
    out = np.zeros((33, dist.shape[0]), np.float32)
    out[row, np.arange(dist.shape[0])] = 1.0
    return out


def make_inputs(cfg, inp):
    C, NS, R, NG, BPC, NPG, DFF, NPHYS = (cfg[k] for k in ("C", "NS", "R", "NG", "BPC", "NPG", "DFF", "NPHYS"))
    T = 8
    NSMP = BPC * T
    NTP = NS * C
    TPC = C // 128
    NT = R * TPC
    WW = C + 128 * NT
    WLEN = ((WW + 127 + 127) // 128) * 128
    f32 = np.float32
    xp = np.asarray(inp["x_prompt"], f32)
    xs = np.asarray(inp["x_sample"], f32)
    ckf = np.asarray(inp["cache_k"], f32)
    cvf = np.asarray(inp["cache_v"], f32)
    pt = np.asarray(inp["page_table"], np.int32)
    tab = np.ascontiguousarray(np.asarray(inp["rel_bias_table"], f32))
    common = dict(
        wg1=np.asarray(inp["ffn1_w_gate"], f32), wu1=np.asarray(inp["ffn1_w_up"], f32), wd1=np.asarray(inp["ffn1_w_down"], f32),
        wg2=np.asarray(inp["ffn2_w_gate"], f32), wu2=np.asarray(inp["ffn2_w_up"], f32), wd2=np.asarray(inp["ffn2_w_down"], f32),
        w_in=np.asarray(inp["w_in"], f32), w_out=np.asarray(inp["w_out"], f32),
        lng=np.ascontiguousarray(np.asarray(inp["ln_g"], f32).reshape(DEPTH, 3, KC, 128).transpose(0, 1, 3, 2)),
        lnb=np.ascontiguousarray(np.asarray(inp["ln_b"], f32).reshape(DEPTH, 3, KC, 128).transpose(0, 1, 3, 2)),
        convw=np.ascontiguousarray(np.asarray(inp["conv_w"], f32).reshape(DEPTH, 3, 4, 128).transpose(0, 3, 2, 1)),
        lamp=np.ascontiguousarray(np.concatenate([np.asarray(inp[k], f32) for k in
                                                  ("lambda_q1", "lambda_k1", "lambda_q2", "lambda_k2")], axis=1)),
        subw=np.asarray(inp["subln_w"], f32), tab=tab,
        antiI=np.ascontiguousarray(np.eye(128, dtype=f32)[::-1]), ident=np.eye(128, dtype=f32),
    )
    xx = np.arange(256)
    sels = np.zeros((33, 512), f32)
    sels[:, 0:256] = _sel_rows(xx + 1)
    sels[:, 256:512] = _sel_rows(xx - 127)
    common["sels"] = sels
    sc = np.asarray(inp["state_conv"], f32)
    in_maps = []
    for core in range(NG * R):
        g, j = core // R, core % R
        m = dict(common)
        cols = [xp[g, (R * s + j) * C:(R * s + j + 1) * C, :] for s in range(NS)]
        b0 = g * R * BPC + j * BPC
        cols.append(xs[b0:b0 + BPC].reshape(NSMP, D))
        m["xT_in"] = np.ascontiguousarray(np.concatenate(cols, axis=0).T)
        m["sconv"] = np.ascontiguousarray(sc[:, b0:b0 + BPC].reshape(DEPTH, BPC, 2, 4, 128).transpose(0, 4, 3, 1, 2))
        m["tabown"] = np.ascontiguousarray(tab[:, j:j + 1])
        x = np.arange(WLEN)
        m["sel"] = _sel_rows(x - 127 + C * j - 128 * (NT - 1))
        m["ptab"] = np.ascontiguousarray(pt[g * R * BPC:(g + 1) * R * BPC].reshape(1, -1))
        rs = np.zeros((128, 4), f32)
        rs[:, j] = 1.0
        m["rsel"] = rs
        hs = np.zeros((128, NS, R * NS), f32)
        for s in range(NS):
            c = R * s + j
            if c > 0:
                pc = c - 1
                hs[:, s, (pc % R) * NS + pc // R] = 1.0
        m["hsel"] = hs.reshape(128, NS * R * NS)
        m["iotap"] = np.arange(128, dtype=f32).reshape(128, 1)
        for l in range(DEPTH):
            m[f"ckv{l}"] = np.ascontiguousarray(np.concatenate(
                [ckf[l, :, :, j, :], cvf[l, :, :, j, :]], axis=-1)).reshape(NPHYS * 128, 256)
        in_maps.append(m)
    return in_maps


def assemble(cfg, res):
    C, NS, R, NG, BPC = (cfg[k] for k in ("C", "NS", "R", "NG", "BPC"))
    T = 8
    NSMP = BPC * T
    NTP = NS * C
    SEQ = R * NS * C
    f32 = np.float32
    yp = np.zeros((NG, SEQ, D), f32)
    ys = np.zeros((NG * R * BPC, T, D), f32)
    kp = np.zeros((DEPTH, NG, SEQ, 4, 128), f32)
    vp = np.zeros((DEPTH, NG, SEQ, 4, 128), f32)
    cpo = np.zeros((DEPTH, NG, 2, 512), f32)
    ks = np.zeros((DEPTH, NG * R * BPC, T, 4, 128), f32)
    vs = np.zeros((DEPTH, NG * R * BPC, T, 4, 128), f32)
    cso = np.zeros((DEPTH, NG * R * BPC, 2, 512), f32)
    for core in range(NG * R):
        g, j = core // R, core % R
        o = res[core]
        y = np.asarray(o["yT"]).T
        b0 = g * R * BPC + j * BPC
        for s in range(NS):
            c = R * s + j
            yp[g, c * C:(c + 1) * C] = y[s * C:(s + 1) * C]
            kp[:, g, c * C:(c + 1) * C] = np.asarray(o["ko"])[:, s * C:(s + 1) * C].reshape(DEPTH, C, 4, 128)
            vp[:, g, c * C:(c + 1) * C] = np.asarray(o["vo"])[:, s * C:(s + 1) * C].reshape(DEPTH, C, 4, 128)
        ys[b0:b0 + BPC] = y[NTP:].reshape(BPC, T, D)
        ks[:, b0:b0 + BPC] = np.asarray(o["ko"])[:, NTP:].reshape(DEPTH, BPC, T, 4, 128)
        vs[:, b0:b0 + BPC] = np.asarray(o["vo"])[:, NTP:].reshape(DEPTH, BPC, T, 4, 128)
        cso[:, b0:b0 + BPC] = np.asarray(o["convs"]).transpose(0, 2, 3, 1)
        if j == R - 1:
            cpo[:, g] = np.asarray(o["convp"])[:, NS - 1].transpose(0, 2, 1)
    return yp, ys, kp, vp, cpo, ks, vs, cso


_NC_CACHE = {}


def kernel(**inputs):
    cfg = FULL
    if "nc" not in _NC_CACHE:
        _NC_CACHE["nc"] = build(cfg)
    nc = _NC_CACHE["nc"]
    in_maps = make_inputs(cfg, inputs)
    res = run_bass_kernel_spmd(nc, in_maps, core_ids=list(range(cfg["NG"] * cfg["R"])))
    return assemble(cfg, res.results)
```
